# Optimizing a Trainium2 kernel written in Bass

```python
import math
import jax, jax.numpy as jnp
from jax import lax
import numpy as np


D_MODEL = 1024
BATCH = 4
SEQ = 8192
DEPTH = 2

A_HEADS = 8
A_HEAD_DIM = 64
A_V_DIM = 2 * A_HEAD_DIM
A_WIDTH = A_HEADS * A_V_DIM
B_HEADS = 8
B_QK_DIM = 64
B_V_DIM = 128
B_WIDTH = B_HEADS * B_V_DIM
CONV_WIDTH = 4
CHUNK = 64
Q_BLOCK = 128
ROPE_THETA = 10000.0
EPS = 1e-6
NEG_INIT = -1e30

SPLITS = (
    A_HEADS * 2 * A_HEAD_DIM,
    A_HEADS * 2 * A_HEAD_DIM,
    A_WIDTH,
    A_WIDTH,
    2 * B_HEADS * B_QK_DIM,
    B_WIDTH,
    B_HEADS,
    B_HEADS,
    B_WIDTH,
    B_WIDTH,
    D_MODEL,
    D_MODEL,
)
D_IN = sum(SPLITS)

kernel_name = 'hybrid_diffattn_mlstm_gated_block'


def rmsnorm(x, g):
    xf = x.astype(jnp.float32)
    y = xf * lax.rsqrt(jnp.mean(xf * xf, axis=-1, keepdims=True) + EPS)
    return y.astype(x.dtype) * g


def rope(x, cos, sin):
    half = x.shape[-1] // 2
    x1, x2 = x[..., :half], x[..., half:]
    cos = cos.astype(x.dtype)
    sin = sin.astype(x.dtype)
    return jnp.concatenate([x1 * cos - x2 * sin, x2 * cos + x1 * sin], axis=-1)


def causal_conv(x, w, b):
    c = x.shape[-1]
    y = lax.conv_general_dilated(
        x, w[:, None, :].astype(x.dtype), window_strides=(1,), padding=[(CONV_WIDTH - 1, 0)],
        dimension_numbers=('NWC', 'WIO', 'NWC'), feature_group_count=c)
    return y + b


def diff_attention(q, k, v, lam):
    bsz, nh, s, _, dh = q.shape
    dv = v.shape[-1]
    nqb = s // Q_BLOCK
    scale = dh ** -0.5
    kpos = jnp.arange(s)

    def block(j):
        start = j * Q_BLOCK
        qb = lax.dynamic_slice_in_dim(q, start, Q_BLOCK, axis=2)
        sc = jnp.einsum('bhqrd,bhkrd->bhrqk', qb, k).astype(jnp.float32) * scale
        qpos = start + jnp.arange(Q_BLOCK)
        causal = kpos[None, :] <= qpos[:, None]
        p = jax.nn.softmax(jnp.where(causal, sc, -jnp.inf), axis=-1)
        a = p[:, :, 0] - lam * p[:, :, 1]
        return jnp.einsum('bhqk,bhkd->bhqd', a.astype(v.dtype), v)

    out = lax.map(block, jnp.arange(nqb))
    return jnp.transpose(out, (1, 2, 0, 3, 4)).reshape(bsz, nh, s, dv)


def mlstm_chunkwise(q, k, v, i_pre, f_pre):
    bsz, nh, s, dk = q.shape
    dv = v.shape[-1]
    nc = s // CHUNK
    f32 = jnp.float32
    q = q.astype(f32).reshape(bsz, nh, nc, CHUNK, dk) * (dk ** -0.5)
    k = k.astype(f32).reshape(bsz, nh, nc, CHUNK, dk)
    v = v.astype(f32).reshape(bsz, nh, nc, CHUNK, dv)
    ig = i_pre.astype(f32).reshape(bsz, nh, nc, CHUNK)
    b = jnp.cumsum(jax.nn.log_sigmoid(f_pre.astype(f32)).reshape(bsz, nh, nc, CHUNK), axis=-1)
    b_last = b[..., -1]
    w_log = b_last[..., None] - b + ig
    m_loc = jnp.max(w_log, axis=-1)
    wk = jnp.exp(w_log - m_loc[..., None])[..., None] * k
    c_loc = jnp.einsum('bhcld,bhcle->bhcde', wk, v)
    n_loc = jnp.sum(wk, axis=3)

    def step(carry, xs):
        c_st, n_st, m_st = carry
        bl, ml, cl, nl = xs
        m_new = jnp.maximum(bl + m_st, ml)
        a = jnp.exp(bl + m_st - m_new)
        e = jnp.exp(ml - m_new)
        c_new = a[..., None, None] * c_st + e[..., None, None] * cl
        n_new = a[..., None] * n_st + e[..., None] * nl
        return (c_new, n_new, m_new), (c_st, n_st, m_st)

    init = (jnp.zeros((bsz, nh, dk, dv), f32), jnp.zeros((bsz, nh, dk), f32), jnp.full((bsz, nh), NEG_INIT, f32))
    xs = (jnp.moveaxis(b_last, 2, 0), jnp.moveaxis(m_loc, 2, 0), jnp.moveaxis(c_loc, 2, 0), jnp.moveaxis(n_loc, 2, 0))
    _, (c_prev, n_prev, m_prev) = lax.scan(step, init, xs)
    c_prev = jnp.moveaxis(c_prev, 0, 2)
    n_prev = jnp.moveaxis(n_prev, 0, 2)
    m_prev = jnp.moveaxis(m_prev, 0, 2)
    causal = jnp.tril(jnp.ones((CHUNK, CHUNK), dtype=bool))
    d_log = jnp.where(causal, b[..., :, None] - b[..., None, :] + ig[..., None, :], -jnp.inf)
    inter_log = b + m_prev[..., None]
    m_t = jnp.maximum(inter_log, jnp.max(d_log, axis=-1))
    scores = jnp.einsum('bhctd,bhcsd->bhcts', q, k) * jnp.exp(d_log - m_t[..., None])
    inter_w = jnp.exp(inter_log - m_t)
    num = jnp.einsum('bhcts,bhcse->bhcte', scores, v) + inter_w[..., None] * jnp.einsum('bhctd,bhcde->bhcte', q, c_prev)
    den = jnp.sum(scores, axis=-1) + inter_w * jnp.einsum('bhctd,bhcd->bhct', q, n_prev)
    h = num / jnp.maximum(jnp.abs(den), jnp.exp(-m_t))[..., None]
    return h.reshape(bsz, nh, s, dv)


def hybrid_layer(x, cos, sin, layer_idx, norm_g, w_in, q_norm_g, k_norm_g, lambda_qk, attn_norm_g, w_out_a,
                 conv_w, conv_b, igate_b, fgate_b, mlstm_norm_g, w_out_b, w_o):
    bsz, s, _ = x.shape
    h = rmsnorm(x, norm_g)
    proj = h @ w_in
    idx = [int(i) for i in np.cumsum(SPLITS)[:-1]]
    aq, ak, av, az, bqk, bv, bi, bf, bo, bz, ga, gb = jnp.split(proj, idx, axis=-1)

    def two_map_heads(t):
        return t.reshape(bsz, s, A_HEADS, 2, A_HEAD_DIM).transpose(0, 2, 1, 3, 4)
    q = rope(rmsnorm(two_map_heads(aq), q_norm_g), cos, sin)
    k = rope(rmsnorm(two_map_heads(ak), k_norm_g), cos, sin)
    v = av.reshape(bsz, s, A_HEADS, A_V_DIM).transpose(0, 2, 1, 3)
    lam_init = 0.8 - 0.6 * math.exp(-0.3 * layer_idx)
    lq1, lk1, lq2, lk2 = lambda_qk[0], lambda_qk[1], lambda_qk[2], lambda_qk[3]
    lam = jnp.exp(jnp.sum(lq1 * lk1)) - jnp.exp(jnp.sum(lq2 * lk2)) + lam_init
    oa = diff_attention(q, k, v, lam)
    oa = rmsnorm(oa, attn_norm_g) * (1.0 - lam_init)
    oa = oa.transpose(0, 2, 1, 3).reshape(bsz, s, A_WIDTH) * jax.nn.silu(az)
    ya = oa @ w_out_a

    qk = jax.nn.silu(causal_conv(bqk, conv_w, conv_b))
    mq, mk = jnp.split(qk, 2, axis=-1)
    mq = mq.reshape(bsz, s, B_HEADS, B_QK_DIM).transpose(0, 2, 1, 3)
    mk = mk.reshape(bsz, s, B_HEADS, B_QK_DIM).transpose(0, 2, 1, 3)
    mv = bv.reshape(bsz, s, B_HEADS, B_V_DIM).transpose(0, 2, 1, 3)
    i_pre = (bi + igate_b).transpose(0, 2, 1)
    f_pre = (bf + fgate_b).transpose(0, 2, 1)
    hb = mlstm_chunkwise(mq, mk, mv, i_pre, f_pre).astype(x.dtype)
    hb = rmsnorm(hb, mlstm_norm_g).transpose(0, 2, 1, 3).reshape(bsz, s, B_WIDTH)
    hb = jax.nn.sigmoid(bo) * hb * jax.nn.silu(bz)
    yb = hb @ w_out_b

    u = jax.nn.sigmoid(ga) * ya + jax.nn.sigmoid(gb) * yb
    return x + u @ w_o


def setup_inputs(seed: int = 0) -> dict:
    key = jax.random.key(seed)
    ks = jax.random.split(key, 16)
    f32 = jnp.float32
    x = jax.random.normal(ks[0], (BATCH, SEQ, D_MODEL), f32)
    offsets = jax.random.randint(ks[1], (BATCH, 1), 0, 4096, dtype=jnp.int32)
    positions = (offsets + jnp.arange(SEQ, dtype=jnp.int32)[None, :]).astype(jnp.int32)
    norm_g = 1.0 + 0.02 * jax.random.normal(ks[2], (DEPTH, D_MODEL), f32)
    w_in = jax.random.normal(ks[3], (DEPTH, D_MODEL, D_IN), f32) * D_MODEL ** -0.5
    q_norm_g = 1.0 + 0.02 * jax.random.normal(ks[4], (DEPTH, A_HEAD_DIM), f32)
    k_norm_g = 1.0 + 0.02 * jax.random.normal(ks[5], (DEPTH, A_HEAD_DIM), f32)
    lambda_qk = 0.1 * jax.random.normal(ks[6], (DEPTH, 4, A_HEAD_DIM), f32)
    attn_norm_g = 1.0 + 0.02 * jax.random.normal(ks[7], (DEPTH, A_V_DIM), f32)
    w_out_a = jax.random.normal(ks[8], (DEPTH, A_WIDTH, D_MODEL), f32) * A_WIDTH ** -0.5
    conv_w = jax.random.normal(ks[9], (DEPTH, CONV_WIDTH, 2 * B_HEADS * B_QK_DIM), f32) * CONV_WIDTH ** -0.5
    conv_b = 0.02 * jax.random.normal(ks[10], (DEPTH, 2 * B_HEADS * B_QK_DIM), f32)
    igate_b = 0.1 * jax.random.normal(ks[11], (DEPTH, B_HEADS), f32)
    fgate_b = jnp.linspace(3.0, 6.0, B_HEADS, dtype=f32)[None, :] + 0.1 * jax.random.normal(ks[12], (DEPTH, B_HEADS), f32)
    mlstm_norm_g = 1.0 + 0.02 * jax.random.normal(ks[13], (DEPTH, B_V_DIM), f32)
    w_out_b = jax.random.normal(ks[14], (DEPTH, B_WIDTH, D_MODEL), f32) * B_WIDTH ** -0.5
    w_o = jax.random.normal(ks[15], (DEPTH, D_MODEL, D_MODEL), f32) * D_MODEL ** -0.5
    return {'x': x, 'positions': positions, 'norm_g': norm_g, 'w_in': w_in, 'q_norm_g': q_norm_g,
            'k_norm_g': k_norm_g, 'lambda_qk': lambda_qk, 'attn_norm_g': attn_norm_g, 'w_out_a': w_out_a,
            'conv_w': conv_w, 'conv_b': conv_b, 'igate_b': igate_b, 'fgate_b': fgate_b,
            'mlstm_norm_g': mlstm_norm_g, 'w_out_b': w_out_b, 'w_o': w_o}


def reference(x, positions, norm_g, w_in, q_norm_g, k_norm_g, lambda_qk, attn_norm_g, w_out_a,
              conv_w, conv_b, igate_b, fgate_b, mlstm_norm_g, w_out_b, w_o):
    inv_freq = ROPE_THETA ** (-jnp.arange(0, A_HEAD_DIM, 2, dtype=jnp.float32) / A_HEAD_DIM)
    ang = positions.astype(jnp.float32)[..., None] * inv_freq
    cos = jnp.cos(ang)[:, None, :, None, :]
    sin = jnp.sin(ang)[:, None, :, None, :]
    for l in range(DEPTH):
        x = hybrid_layer(x, cos, sin, l, norm_g[l], w_in[l], q_norm_g[l], k_norm_g[l], lambda_qk[l],
                         attn_norm_g[l], w_out_a[l], conv_w[l], conv_b[l], igate_b[l], fgate_b[l],
                         mlstm_norm_g[l], w_out_b[l], w_o[l])
    return x
```

```python
import math
from contextlib import ExitStack

import numpy as np
import concourse.bass as bass
import concourse.mybir as mybir
from concourse.bass_utils import run_bass_kernel_spmd

F32 = mybir.dt.float32
BF16 = mybir.dt.bfloat16
I32 = mybir.dt.int32
AF = mybir.ActivationFunctionType
ALU = mybir.AluOpType
AX = mybir.AxisListType

D = 1024
NH = 8
EPS = 1e-6
DIN = 10256
SELF_WAIT = True
DTSZ = {F32: 4, BF16: 2, I32: 4}


class Res:
    __slots__ = ("w", "r")

    def __init__(self):
        self.w = {}
        self.r = {}


class Buf(Res):
    __slots__ = ("a",)

    def __init__(self, ap):
        super().__init__()
        self.a = ap


class Chan:
    def __init__(self, sem, key):
        self.sem = sem
        self.key = key
        self.cnt = 0


class MK:
    ENG = ("pe", "act", "dve", "pool", "sp")

    def __init__(self, nc, es):
        self.nc = nc
        self.es = es
        self.q = {e: [] for e in self.ENG}
        self.sem = {e: es.enter_context(nc.semaphore("sem_" + e)) for e in self.ENG}
        self.cnt = {e: 0 for e in self.ENG}
        self.seen = {e: {} for e in self.ENG}
        self.chans = []
        self.ninst = 0

    def chan(self):
        k = "ch%d" % len(self.chans)
        c = Chan(self.es.enter_context(self.nc.semaphore(k)), k)
        self.chans.append(c)
        return c

    def _collect(self, eng, reads, writes, waw):
        need = {}
        seen = self.seen[eng]

        def add(d):
            for k, (s, v, src) in d.items():
                if src == eng and (eng == "pe" or not SELF_WAIT):
                    continue
                if seen.get(k, 0) >= v:
                    continue
                if k not in need or need[k][1] < v:
                    need[k] = (s, v)

        for r in reads:
            add(r.w)
        for w in writes:
            add(w.r)
            if waw:
                add(w.w)
        for k, (s, v) in need.items():
            seen[k] = v
        return list(need.values())

    def _mark(self, key, tok, reads, writes, waw):
        for r in reads:
            r.r[key] = tok
        for w in writes:
            if waw:
                w.w = {key: tok}
            else:
                w.w[key] = tok
            w.r = {}

    def op(self, eng, fn, reads=(), writes=(), waw=True, signal=True):
        toks = self._collect(eng, reads, writes, waw)
        sem = self.sem[eng]
        if signal:
            self.cnt[eng] += 1
            val = self.cnt[eng]
        else:
            val = self.cnt[eng] + 1
        self.q[eng].append((toks, fn, sem if signal else None, 1))
        self._mark(eng, (sem, val, eng), reads, writes, waw)
        self.ninst += 1

    def dma(self, q, ch, out, in_, reads=(), writes=(), waw=True):
        toks = self._collect(q, reads, writes, waw)
        ch.cnt += 16
        self.q[q].append((toks, lambda e: e.dma_start(out=out, in_=in_), ch.sem, 16))
        self._mark(ch.key, (ch.sem, ch.cnt, "dma"), reads, writes, waw)
        self.ninst += 1

    def barrier(self):
        allt = [(self.sem[e], self.cnt[e], e) for e in self.ENG if self.cnt[e] > 0]
        allt += [(c.sem, c.cnt, c.key) for c in self.chans if c.cnt > 0]
        for eng in self.ENG:
            toks = []
            for (s, v, k) in allt:
                if k == eng:
                    continue
                if self.seen[eng].get(k, 0) >= v:
                    continue
                self.seen[eng][k] = v
                toks.append((s, v))
            if toks:
                self.q[eng].append((toks, None, None, 0))

    def emit(self):
        nc = self.nc
        q = self.q

        def run(e, lst):
            for toks, fn, sem, inc in lst:
                for (s, v) in toks:
                    e.wait_ge(s, v)
                if fn is None:
                    continue
                ins = fn(e)
                if sem is not None:
                    ins.then_inc(sem, inc)

        with nc.Block() as block:
            @block.sync
            def _(e):
                run(e, q["sp"])

            @block.tensor
            def _(e):
                run(e, q["pe"])

            @block.scalar
            def _(e):
                run(e, q["act"])

            @block.vector
            def _(e):
                run(e, q["dve"])

            @block.gpsimd
            def _(e):
                run(e, q["pool"])


class Arena:
    def __init__(self, ap, words):
        self.ap = ap
        self.words = words
        self.top = 0

    def tile(self, free, dt):
        n = int(np.prod(free))
        w = (n * DTSZ[dt] + 3) // 4
        w = (w + 7) // 8 * 8
        assert self.top + w <= self.words, ("arena overflow", self.top, w, self.words)
        v = self.ap[:, self.top:self.top + w]
        self.top += w
        if dt != F32:
            v = v.bitcast(dt)
        v = v[:, 0:n]
        if len(free) == 2:
            v = v.rearrange("p (a b) -> p a b", a=free[0])
        elif len(free) == 3:
            v = v.rearrange("p (a b c) -> p a b c", a=free[0], b=free[1])
        return Buf(v)


def rsqrt_act(mk, src, dst, n):
    mk.op("act", lambda e: e.activation(out=dst.a, in_=src.a, func=AF.Ln, scale=1.0 / n, bias=EPS), [src], [dst])
    mk.op("act", lambda e: e.activation(out=dst.a, in_=dst.a, func=AF.Exp, scale=-0.5), [dst], [dst])


def bc(ap, axis, shape):
    return ap.unsqueeze(axis).to_broadcast(list(shape))


def build(S, L=2, lam_inits=None, debug=False):
    NT = S // 128
    TS = min(1024, S)
    NST = S // TS
    NTS = TS // 128
    NQ = S // 512
    nc = bass.Bass("TRN2", target_bir_lowering=False)
    es = ExitStack()

    def din(name, shape, dt=F32):
        return nc.dram_tensor(name, list(shape), dt, kind="ExternalInput").ap()

    def dscr(name, shape, dt):
        return nc.dram_tensor(name, list(shape), dt, kind="Internal").ap()

    x_in = din("x", [S, D])
    pos_in = din("pos", [128, NT], I32)
    cst_ident = din("c_ident", [128, 128])
    cst_tri = din("c_tri", [128, 128])
    cst_freq = din("c_freq", [128, 32])
    w_in = din("w_in", [L, D, DIN])
    w_out = din("w_out", [L, 3, D, D])
    norm_g = din("norm_g", [L, 128, 8])
    q_g = din("q_g", [L, 64])
    k_g = din("k_g", [L, 64])
    lam_qk = din("lam_qk", [L, 256])
    attn_g = din("attn_g", [L, 128, 1])
    ml_g = din("ml_g", [L, 128, 1])
    conv_w = din("conv_w", [L, 4, 1024])
    conv_b = din("conv_b", [L, 1024])
    gate_b = din("gate_b", [L, 16])
    out_d = nc.dram_tensor("out", [S, D], F32, kind="ExternalOutput").ap()

    wbf = [dscr("wbf%d" % l, [27, 128, 8, 512], BF16) for l in range(L)]
    wobf = [dscr("wobf%d" % l, [3, 128, 8, 1024], BF16) for l in range(L)]
    qT_scr = dscr("qT_scr", [8, 128, S], BF16)
    kT_scr = dscr("kT_scr", [8, 128, S], BF16)
    v_scr = dscr("v_scr", [S, D], BF16)
    zsa_scr = dscr("zsa_scr", [S, D], BF16)
    mqT_scr = dscr("mqT_scr", [4, 128, S], BF16)
    mkT_scr = dscr("mkT_scr", [4, 128, S], BF16)
    mk_scr = dscr("mk_scr", [S, 512], BF16)
    mv_scr = dscr("mv_scr", [S, D], BF16)
    og_scr = dscr("og_scr", [S, D], BF16)
    sga_scr = dscr("sga_scr", [S, D], BF16)
    sgb_scr = dscr("sgb_scr", [S, D], BF16)
    oa_scr = dscr("oa_scr", [S, D], BF16)
    hb_scr = dscr("hb_scr", [S, D], BF16)
    x_mid = dscr("x_mid", [S, D], F32)
    R = {n: Res() for n in ["x_in", "wbf0", "wbf1", "wobf0", "wobf1", "qT", "kT", "v", "zsa", "mqT", "mkT", "mk", "mv",
                            "og", "sga", "sgb", "oa", "hb", "x_mid", "out", "const"]}

    AW = 50 * 1024
    arena_t = es.enter_context(nc.sbuf_tensor("arena", [128, AW], F32))
    ps_t = es.enter_context(nc.psum_tensor("ps", [128, 4096], F32))
    mk = MK(nc, es)
    A = Arena(arena_t, AW)

    def psb(c0, ncol, dt=F32, free=None):
        v = ps_t[:, c0:c0 + ncol]
        if dt != F32:
            v = v.bitcast(dt)
        if free is not None:
            if len(free) == 2:
                v = v.rearrange("p (a b) -> p a b", a=free[0])
        return Buf(v)

    ch_ld = [mk.chan() for _ in range(6)]
    ch_st = [mk.chan() for _ in range(6)]
    ldi = [0]
    sti = [0]

    def load(out, in_, reads, writes, waw=True, ch=None):
        if ch is None:
            ch = ch_ld[ldi[0] % len(ch_ld)]
            ldi[0] += 1
        mk.dma("sp", ch, out, in_, reads=reads, writes=writes, waw=waw)

    def store(out, in_, reads, writes, waw=False, ch=None):
        if ch is None:
            ch = ch_st[sti[0] % len(ch_st)]
            sti[0] += 1
        mk.dma("pool", ch, out, in_, reads=reads, writes=writes, waw=waw)

    ident_f = A.tile([128], F32)
    ident_b = A.tile([128], BF16)
    tri_f = A.tile([128], F32)
    tri_b = A.tile([128], BF16)
    ones_f = A.tile([128], F32)
    mhalf = A.tile([16], F32)
    cosT = A.tile([NT, 32], F32)
    sinT = A.tile([NT, 32], F32)
    Graw = A.tile([NT, 16], F32)
    PERS = A.top

    load(ident_f.a, cst_ident, [R["const"]], [ident_f])
    load(tri_f.a, cst_tri, [R["const"]], [tri_f])
    mk.op("dve", lambda e: e.tensor_copy(out=ident_b.a, in_=ident_f.a), [ident_f], [ident_b])
    mk.op("dve", lambda e: e.tensor_copy(out=tri_b.a, in_=tri_f.a), [tri_f], [tri_b])
    mk.op("pool", lambda e: e.memset(ones_f.a, 1.0), [], [ones_f])
    mk.op("pool", lambda e: e.memset(mhalf.a, -0.5), [], [mhalf])

    m0 = A.top
    freq = A.tile([32], F32)
    posi = A.tile([NT], I32)
    posf = A.tile([NT], F32)
    u = A.tile([NT, 32], F32)
    ui = A.tile([NT, 32], I32)
    uf = A.tile([NT, 32], F32)
    load(freq.a, cst_freq, [R["const"]], [freq])
    load(posi.a, pos_in, [R["const"]], [posi])
    mk.op("dve", lambda e: e.tensor_copy(out=posf.a, in_=posi.a), [posi], [posf])
    mk.op("dve", lambda e: e.tensor_tensor(out=u.a, in0=bc(posf.a, 2, [128, NT, 32]), in1=bc(freq.a, 1, [128, NT, 32]),
                                           op=ALU.mult), [posf, freq], [u])
    for tab, off in ((sinT, 0.0), (cosT, 0.25)):
        if off != 0.0:
            mk.op("dve", lambda e: e.tensor_scalar(out=u.a, in0=u.a, scalar1=0.25, scalar2=None, op0=ALU.add), [u], [u])
        mk.op("dve", lambda e: e.tensor_copy(out=ui.a, in_=u.a), [u], [ui])
        mk.op("dve", lambda e: e.tensor_copy(out=uf.a, in_=ui.a), [ui], [uf])
        mk.op("dve", lambda e: e.tensor_tensor(out=uf.a, in0=u.a, in1=uf.a, op=ALU.subtract), [u, uf], [uf])
        mk.op("dve", lambda e: e.tensor_scalar(out=uf.a, in0=uf.a, scalar1=-0.49999, scalar2=0.49999, op0=ALU.max,
                                               op1=ALU.min), [uf], [uf])
        mk.op("act", lambda e, tab=tab: e.activation(out=tab.a, in_=uf.a, func=AF.Sin, scale=2.0 * math.pi), [uf], [tab])
    mk.barrier()
    A.top = m0

    def phase0(l):
        m = A.top
        gcol = A.tile([8], F32)
        gA = A.tile([1], F32)
        gB = A.tile([1], F32)
        cwb = A.tile([4, 1024], F32)
        wt = [A.tile([2048], F32) for _ in range(2)]
        wb = [A.tile([2048], BF16) for _ in range(2)]
        wb4 = [A.tile([1024], BF16) for _ in range(2)]
        Rw, Rwo = R["wbf%d" % l], R["wobf%d" % l]
        load(gcol.a, norm_g[l], [R["const"]], [gcol])
        load(gA.a, attn_g[l], [R["const"]], [gA])
        load(gB.a, ml_g[l], [R["const"]], [gB])
        load(cwb.a, conv_w[l].rearrange("j c -> (j c)").partition_broadcast(128).rearrange("p (j c) -> p j c", j=4),
             [R["const"]], [cwb])
        mk.op("dve", lambda e: e.tensor_scalar(out=gA.a, in0=gA.a, scalar1=float(1.0 - lam_inits[l]), scalar2=None,
                                               op0=ALU.mult), [gA], [gA])
        n = 0
        for kc in range(8):
            rows = slice(kc * 128, (kc + 1) * 128)
            gs = gcol.a[:, kc:kc + 1]
            for ci in range(5):
                t, b = wt[n % 2], wb[n % 2]
                load(t.a, w_in[l, rows, ci * 2048:(ci + 1) * 2048], [R["const"]], [t])
                if ci < 4:
                    if n % 2 == 0:
                        mk.op("dve", lambda e, t=t, b=b, gs=gs: e.tensor_scalar(out=b.a, in0=t.a, scalar1=gs, scalar2=None,
                                                                             op0=ALU.mult), [t, gcol], [b])
                    else:
                        mk.op("act", lambda e, t=t, b=b, gs=gs: e.activation(out=b.a, in_=t.a, func=AF.Copy, scale=gs),
                              [t, gcol], [b])
                    store(wbf[l][ci * 4:ci * 4 + 4, :, kc, :].rearrange("b p c -> p b c"),
                          b.a.rearrange("p (b c) -> p b c", b=4), [b], [Rw])
                else:
                    mk.op("act", lambda e, t=t, b=b, gs=gs: e.activation(out=b.a[:, 0:1024], in_=t.a[:, 0:1024], func=AF.Copy,
                                                                       scale=gs), [t, gcol], [b])
                    store(wbf[l][16:18, :, kc, :].rearrange("b p c -> p b c"),
                          b.a[:, 0:1024].rearrange("p (b c) -> p b c", b=2), [b], [Rw])
                    mk.op("dve", lambda e, t=t, gs=gs: e.tensor_scalar(out=t.a[:, 1024:2048], in0=t.a[:, 1024:2048],
                                                                   scalar1=gs, scalar2=None, op0=ALU.mult), [t, gcol], [t])
                    for j in range(4):
                        b4 = wb4[j % 2]
                        eng = "dve" if j % 2 == 0 else "pool"
                        mk.op(eng, lambda e, t=t, b4=b4, j=j: e.tensor_tensor(out=b4.a, in0=t.a[:, 1024:2048],
                                                                             in1=cwb.a[:, j, :], op=ALU.mult), [t, cwb], [b4])
                        store(wbf[l][18 + j, :, kc, :], b4.a[:, 0:512], [b4], [Rw])
                        store(wbf[l][22 + j, :, kc, :], b4.a[:, 512:1024], [b4], [Rw])
                n += 1
            t, b = wt[n % 2], wb[n % 2]
            load(t.a[:, 0:16], w_in[l, rows, 10240:10256], [R["const"]], [t])
            mk.op("dve", lambda e, t=t, b=b, gs=gs: e.tensor_scalar(out=b.a[:, 0:16], in0=t.a[:, 0:16], scalar1=gs,
                                                                 scalar2=None, op0=ALU.mult), [t, gcol], [b])
            store(wbf[l][26, :, kc, 0:16], b.a[:, 0:16], [b], [Rw])
            n += 1
            for mi in range(3):
                t, b = wt[n % 2], wb[n % 2]
                load(t.a[:, 0:1024], w_out[l, mi, rows, :], [R["const"]], [t])
                if mi == 2:
                    mk.op("act", lambda e, t=t, b=b: e.copy(out=b.a[:, 0:1024], in_=t.a[:, 0:1024]), [t], [b])
                else:
                    gg = gA if mi == 0 else gB
                    mk.op("dve", lambda e, t=t, b=b, gg=gg: e.tensor_scalar(out=b.a[:, 0:1024], in0=t.a[:, 0:1024],
                                                                         scalar1=gg.a[:, 0:1], scalar2=None, op0=ALU.mult),
                          [t, gg], [b])
                store(wobf[l][mi, :, kc, :], b.a[:, 0:1024], [b], [Rwo])
                n += 1
        mk.barrier()
        A.top = m

    def phase1(l):
        m = A.top
        xsrc, Rx = (x_in, R["x_in"]) if l == 0 else (x_mid, R["x_mid"])
        Rw = R["wbf%d" % l]
        xt = [A.tile([1024], F32) for _ in range(2)]
        xb = [A.tile([1024], BF16) for _ in range(2)]
        junk = A.tile([1024], BF16)
        ss = [A.tile([1], F32) for _ in range(2)]
        rstd = [A.tile([1], F32) for _ in range(2)]
        hT = [A.tile([8, 3 + TS], BF16) for _ in range(2)]
        NWB = 6
        wblk = [A.tile([8, 512], BF16) for _ in range(NWB)]
        stT = [A.tile([4, TS], BF16) for _ in range(2)]
        stN = [A.tile([NTS, 512], BF16) for _ in range(2)]
        sbo = A.tile([NTS, 512], BF16)
        ta = [A.tile([512], F32) for _ in range(2)]
        tb2 = [A.tile([512], F32) for _ in range(2)]
        qn = [A.tile([8, 64], F32) for _ in range(2)]
        t1 = [A.tile([8, 32], F32) for _ in range(2)]
        t2 = [A.tile([8, 32], F32) for _ in range(2)]
        t3 = [A.tile([8, 32], F32) for _ in range(2)]
        t4 = [A.tile([8, 32], F32) for _ in range(2)]
        qrot = [A.tile([8, 64], BF16) for _ in range(2)]
        ybf = [A.tile([512], BF16) for _ in range(2)]
        s8 = [A.tile([8], F32) for _ in range(2)]
        RT = {"q": A.tile([NTS, 4, 32], F32), "k": A.tile([NTS, 4, 32], F32)}
        gq = {"q": A.tile([64], F32), "k": A.tile([64], F32)}
        cbb = A.tile([1024], F32)
        load(gq["q"].a, q_g[l].partition_broadcast(128), [R["const"]], [gq["q"]])
        load(gq["k"].a, k_g[l].partition_broadcast(128), [R["const"]], [gq["k"]])
        load(cbb.a, conv_b[l].partition_broadcast(128), [R["const"]], [cbb])
        Pm = [psb(i * 512, 512) for i in range(4)]
        psX = [psb(2048 + i * 512, 512, BF16, [8, 128]) for i in range(2)]
        psQ = [psb(3072 + i * 256, 256, BF16, [4, 128]) for i in range(2)]
        cnt = {"p": 0, "w": 0, "e": 0, "q": 0, "stT": 0, "stN": 0}

        blocks = [("qk", "q", 0, 0), ("qk", "q", 1, 1), ("qk", "k", 0, 2), ("qk", "k", 1, 3),
                  ("copy", "v", 0, 4), ("copy", "v", 1, 5), ("silu", "zsa", 0, 6), ("silu", "zsa", 1, 7),
                  ("copy", "mv", 0, 8), ("copy", "mv", 1, 9),
                  ("sigk", None, 0, 10), ("silum", "og", 0, 12), ("sigk", None, 1, 11), ("silum", "og", 1, 13),
                  ("sig", "sga", 0, 14), ("sig", "sga", 1, 15), ("sig", "sgb", 0, 16), ("sig", "sgb", 1, 17),
                  ("conv", "q", 0, 18), ("conv", "k", 0, 22), ("gates", None, 0, 26)]
        scr = {"v": v_scr, "zsa": zsa_scr, "mv": mv_scr, "og": og_scr, "sga": sga_scr, "sgb": sgb_scr}

        for st in range(NST):
            h = hT[st % 2]
            tok0 = st * TS
            if st == 0:
                mk.op("pool", lambda e, h=h: e.memset(h.a[:, :, 0:3], 0.0), [], [h])
            else:
                hp = hT[(st - 1) % 2]
                mk.op("pool", lambda e, h=h, hp=hp: e.tensor_copy(out=h.a[:, :, 0:3], in_=hp.a[:, :, TS:TS + 3]), [hp], [h])
            for i in range(NTS):
                x_, xb_, ss_, rs_ = xt[i % 2], xb[i % 2], ss[i % 2], rstd[i % 2]
                pX = psX[i % 2]
                load(x_.a, xsrc[tok0 + i * 128: tok0 + (i + 1) * 128, :], [Rx], [x_])
                mk.op("act", lambda e, x_=x_, ss_=ss_: e.activation(out=junk.a, in_=x_.a, func=AF.Square, accum_out=ss_.a),
                      [x_], [junk, ss_])
                rsqrt_act(mk, ss_, rs_, D)
                mk.op("dve", lambda e, x_=x_, xb_=xb_, rs_=rs_: e.tensor_scalar(out=xb_.a, in0=x_.a, scalar1=rs_.a[:, 0:1],
                                                                             scalar2=None, op0=ALU.mult), [x_, rs_], [xb_])
                for kc in range(8):
                    mk.op("pe", lambda e, pX=pX, xb_=xb_, kc=kc: e.transpose(out=pX.a[:, kc, :],
                                                                           in_=xb_.a[:, kc * 128:(kc + 1) * 128],
                                                                           identity=ident_b.a),
                          [xb_, ident_b], [pX], waw=(kc == 0), signal=(kc == 7))
                mk.op("act", lambda e, h=h, pX=pX, i=i: e.copy(out=h.a[:, :, 3 + i * 128: 3 + (i + 1) * 128], in_=pX.a),
                      [pX], [h], waw=False)
            cs = cosT.a[:, st * NTS:(st + 1) * NTS, :]
            sn = sinT.a[:, st * NTS:(st + 1) * NTS, :]
            for wh in ("q", "k"):
                g1 = bc(gq[wh].a[:, 0:32], 1, [128, NTS, 32])
                g2 = bc(gq[wh].a[:, 32:64], 1, [128, NTS, 32])
                rt = RT[wh]
                for k_, (a_, g_) in enumerate(((cs, g1), (sn, g2), (cs, g2), (sn, g1))):
                    mk.op("pool", lambda e, rt=rt, k_=k_, a_=a_, g_=g_: e.tensor_tensor(out=rt.a[:, :, k_, :], in0=a_, in1=g_,
                                                                                   op=ALU.mult),
                          [cosT, sinT, gq[wh]], [rt], waw=(k_ == 0))

            for (role, name, half, bidx) in blocks:
                nsub = 4 if role == "conv" else 1
                ws = []
                for j in range(nsub):
                    w = wblk[cnt["w"] % NWB]
                    cnt["w"] += 1
                    if role == "gates":
                        load(w.a[:, :, 0:16], wbf[l][bidx, :, :, 0:16], [Rw], [w])
                    else:
                        load(w.a, wbf[l][bidx + j], [Rw], [w])
                    ws.append(w)
                ncol = 16 if role == "gates" else 512
                sT = sN = None
                if role in ("qk", "conv"):
                    sT = stT[cnt["stT"] % 2]
                    cnt["stT"] += 1
                if role in ("copy", "silu", "silum", "sig") or (role == "conv" and name == "k"):
                    sN = stN[cnt["stN"] % 2]
                    cnt["stN"] += 1
                for i in range(NTS):
                    P = Pm[cnt["p"] % 4]
                    cnt["p"] += 1
                    nmm = nsub * 8
                    c = 0
                    for j in range(nsub):
                        for kc in range(8):
                            off = (j if role == "conv" else 3) + i * 128
                            mk.op("pe", lambda e, P=P, h=h, w=ws[j], kc=kc, off=off, c=c, nmm=nmm, ncol=ncol: e.matmul(
                                P.a[:, 0:ncol], lhsT=h.a[:, kc, off:off + 128], rhs=w.a[:, kc, 0:ncol],
                                start=(c == 0), stop=(c == nmm - 1)), [h, ws[j]], [P], waw=(c == 0), signal=(c == nmm - 1))
                            c += 1
                    e_ = cnt["e"] % 2
                    cnt["e"] += 1
                    if role == "copy":
                        eng = "act" if i % 2 == 0 else "dve"
                        if eng == "act":
                            mk.op("act", lambda e, P=P, sN=sN, i=i: e.copy(out=sN.a[:, i, :], in_=P.a), [P], [sN], waw=False)
                        else:
                            mk.op("dve", lambda e, P=P, sN=sN, i=i: e.tensor_copy(out=sN.a[:, i, :], in_=P.a), [P], [sN],
                                  waw=False)
                    elif role == "sig":
                        mk.op("act", lambda e, P=P, sN=sN, i=i: e.activation(out=sN.a[:, i, :], in_=P.a, func=AF.Sigmoid),
                              [P], [sN], waw=False)
                    elif role == "sigk":
                        mk.op("act", lambda e, P=P, i=i: e.activation(out=sbo.a[:, i, :], in_=P.a, func=AF.Sigmoid),
                              [P], [sbo], waw=False)
                    elif role == "silu":
                        tA = ta[e_]
                        mk.op("act", lambda e, P=P, tA=tA: e.activation(out=tA.a, in_=P.a, func=AF.Sigmoid), [P], [tA])
                        mk.op("dve", lambda e, P=P, tA=tA, sN=sN, i=i: e.tensor_tensor(out=sN.a[:, i, :], in0=P.a, in1=tA.a,
                                                                                op=ALU.mult), [P, tA], [sN], waw=False)
                    elif role == "silum":
                        tA, tB = ta[e_], tb2[e_]
                        mk.op("act", lambda e, P=P, tA=tA: e.activation(out=tA.a, in_=P.a, func=AF.Sigmoid), [P], [tA])
                        mk.op("dve", lambda e, P=P, tA=tA, tB=tB: e.tensor_tensor(out=tB.a, in0=P.a, in1=tA.a, op=ALU.mult),
                              [P, tA], [tB])
                        mk.op("pool", lambda e, tB=tB, sN=sN, i=i: e.tensor_tensor(out=sN.a[:, i, :], in0=tB.a,
                                                                                 in1=sbo.a[:, i, :], op=ALU.mult),
                              [tB, sbo], [sN], waw=False)
                    elif role == "gates":
                        ti = st * NTS + i
                        mk.op("dve", lambda e, P=P, ti=ti: e.tensor_copy(out=Graw.a[:, ti, :], in_=P.a[:, 0:16]), [P], [Graw],
                              waw=False)
                    elif role == "qk":
                        tA, s8_, qn_, qr_ = ta[e_], s8[e_], qn[e_], qrot[e_]
                        a1, a2, a3, a4 = t1[e_], t2[e_], t3[e_], t4[e_]
                        rt = RT[name]
                        pQ = psQ[cnt["q"] % 2]
                        cnt["q"] += 1
                        mk.op("act", lambda e, P=P, tA=tA: e.activation(out=tA.a, in_=P.a, func=AF.Square), [P], [tA])
                        mk.op("dve", lambda e, tA=tA, s8_=s8_: e.tensor_reduce(
                            out=s8_.a, in_=tA.a.rearrange("p (g d) -> p g d", g=8), axis=AX.X, op=ALU.add), [tA], [s8_])
                        rsqrt_act(mk, s8_, s8_, 64)
                        mk.op("dve", lambda e, P=P, s8_=s8_, qn_=qn_: e.tensor_tensor(
                            out=qn_.a, in0=P.a.rearrange("p (g d) -> p g d", g=8), in1=bc(s8_.a, 2, [128, 8, 64]),
                            op=ALU.mult), [P, s8_], [qn_])
                        x1 = qn_.a[:, :, 0:32]
                        x2 = qn_.a[:, :, 32:64]

                        def tbv(k_, rt=rt, i=i):
                            return bc(rt.a[:, i, k_, :], 1, [128, 8, 32])

                        mk.op("dve", lambda e, a1=a1, x1=x1, v=tbv(0): e.tensor_tensor(out=a1.a, in0=x1, in1=v, op=ALU.mult),
                              [qn_, rt], [a1])
                        mk.op("dve", lambda e, a2=a2, x2=x2, v=tbv(1): e.tensor_tensor(out=a2.a, in0=x2, in1=v, op=ALU.mult),
                              [qn_, rt], [a2])
                        mk.op("dve", lambda e, a1=a1, a2=a2, qr_=qr_: e.tensor_tensor(out=qr_.a[:, :, 0:32], in0=a1.a, in1=a2.a,
                                                                                 op=ALU.subtract), [a1, a2], [qr_])
                        mk.op("pool", lambda e, a3=a3, x2=x2, v=tbv(2): e.tensor_tensor(out=a3.a, in0=x2, in1=v, op=ALU.mult),
                              [qn_, rt], [a3])
                        mk.op("pool", lambda e, a4=a4, x1=x1, v=tbv(3): e.tensor_tensor(out=a4.a, in0=x1, in1=v, op=ALU.mult),
                              [qn_, rt], [a4])
                        mk.op("pool", lambda e, a3=a3, a4=a4, qr_=qr_: e.tensor_tensor(out=qr_.a[:, :, 32:64], in0=a3.a,
                                                                                  in1=a4.a, op=ALU.add), [a3, a4], [qr_],
                              waw=False)
                        qf = qr_.a.rearrange("p g d -> p (g d)")
                        for hh in range(4):
                            mk.op("pe", lambda e, pQ=pQ, qf=qf, hh=hh: e.transpose(out=pQ.a[:, hh, :],
                                                                                in_=qf[:, hh * 128:(hh + 1) * 128],
                                                                                identity=ident_b.a),
                                  [qr_, ident_b], [pQ], waw=(hh == 0), signal=(hh == 3))
                        mk.op("act", lambda e, pQ=pQ, sT=sT, i=i: e.copy(out=sT.a[:, :, i * 128:(i + 1) * 128], in_=pQ.a),
                              [pQ], [sT], waw=False)
                    elif role == "conv":
                        tA, tB = ta[e_], tb2[e_]
                        pQ = psQ[cnt["q"] % 2]
                        cnt["q"] += 1
                        coff = 0 if name == "q" else 512
                        yb_ = ybf[e_] if name == "q" else None
                        yv = yb_.a if name == "q" else sN.a[:, i, :]
                        yres = yb_ if name == "q" else sN
                        mk.op("dve", lambda e, P=P, tA=tA, coff=coff: e.tensor_tensor(out=tA.a, in0=P.a,
                                                                                   in1=cbb.a[:, coff:coff + 512], op=ALU.add),
                              [P, cbb], [tA])
                        mk.op("act", lambda e, tA=tA, tB=tB: e.activation(out=tB.a, in_=tA.a, func=AF.Sigmoid), [tA], [tB])
                        sc = 0.125 if name == "q" else 1.0
                        mk.op("dve", lambda e, tA=tA, tB=tB, yv=yv, sc=sc: e.scalar_tensor_tensor(
                            out=yv, in0=tA.a, scalar=sc, in1=tB.a, op0=ALU.mult, op1=ALU.mult), [tA, tB], [yres],
                            waw=(name == "q"))
                        for hh in range(4):
                            mk.op("pe", lambda e, pQ=pQ, yv=yv, hh=hh: e.transpose(out=pQ.a[:, hh, :],
                                                                                in_=yv[:, hh * 128:(hh + 1) * 128],
                                                                                identity=ident_b.a),
                                  [yres, ident_b], [pQ], waw=(hh == 0), signal=(hh == 3))
                        mk.op("act", lambda e, pQ=pQ, sT=sT, i=i: e.copy(out=sT.a[:, :, i * 128:(i + 1) * 128], in_=pQ.a),
                              [pQ], [sT], waw=False)
                tsl = slice(tok0, tok0 + TS)
                if role == "qk":
                    dst, rn = (qT_scr, "qT") if name == "q" else (kT_scr, "kT")
                    store(dst[half * 4:(half + 1) * 4, :, tsl].rearrange("h p t -> p h t"), sT.a, [sT], [R[rn]])
                elif role == "conv":
                    dst, rn = (mqT_scr, "mqT") if name == "q" else (mkT_scr, "mkT")
                    store(dst[:, :, tsl].rearrange("h p t -> p h t"), sT.a, [sT], [R[rn]])
                    if name == "k":
                        store(mk_scr[tsl, :].rearrange("(i p) c -> p i c", p=128), sN.a, [sN], [R["mk"]])
                elif role in ("copy", "silu", "silum", "sig"):
                    store(scr[name][tsl, half * 512:(half + 1) * 512].rearrange("(i p) c -> p i c", p=128), sN.a,
                          [sN], [R[name]])
        mk.barrier()
        A.top = m

    def phase2(l):
        m = A.top
        gtb = A.tile([16], F32)
        lamb = A.tile([256], F32)
        G = A.tile([NT, 16], F32)
        SPl = A.tile([NT, 8], F32)
        T1 = A.tile([NT, 8], F32)
        Aal = A.tile([NT, 8], F32)
        Kp = A.tile([NT, 8], F32)
        Binv = A.tile([NT, 8], F32)
        Dec = A.tile([NT, 8], F32)
        neglam = A.tile([1], F32)
        ltmp = A.tile([64], F32)
        ls = A.tile([2], F32)
        load(gtb.a, gate_b[l].partition_broadcast(128), [R["const"]], [gtb])
        load(lamb.a, lam_qk[l].partition_broadcast(128), [R["const"]], [lamb])
        nbp = psb(0, NT * 8)
        nblp = psb(512, NT * 8)
        mk.op("dve", lambda e: e.tensor_tensor(out=G.a, in0=Graw.a, in1=bc(gtb.a, 1, [128, NT, 16]), op=ALU.add),
              [Graw, gtb], [G])
        mk.op("act", lambda e: e.activation(out=SPl.a, in_=G.a[:, :, 8:16], func=AF.Exp, scale=-1.0), [G], [SPl])
        mk.op("act", lambda e: e.activation(out=SPl.a, in_=SPl.a, func=AF.Ln, bias=1.0), [SPl], [SPl])
        spf = SPl.a.rearrange("p c h -> p (c h)")
        mk.op("pe", lambda e: e.matmul(nbp.a, lhsT=tri_f.a, rhs=spf, start=True, stop=True), [tri_f, SPl], [nbp])
        mk.op("pe", lambda e: e.matmul(nblp.a, lhsT=ones_f.a, rhs=spf, start=True, stop=True), [ones_f, SPl], [nblp])
        nb3 = nbp.a.rearrange("p (c h) -> p c h", h=8)
        nbl3 = nblp.a.rearrange("p (c h) -> p c h", h=8)
        mk.op("dve", lambda e: e.tensor_tensor(out=T1.a, in0=nb3, in1=G.a[:, :, 0:8], op=ALU.add), [nbp, G], [T1])
        mk.op("act", lambda e: e.activation(out=Aal.a, in_=T1.a, func=AF.Exp), [T1], [Aal])
        mk.op("dve", lambda e: e.tensor_tensor(out=T1.a, in0=T1.a, in1=nbl3, op=ALU.subtract), [T1, nblp], [T1])
        mk.op("act", lambda e: e.activation(out=Kp.a, in_=T1.a, func=AF.Exp), [T1], [Kp])
        mk.op("act", lambda e: e.activation(out=Binv.a, in_=nb3, func=AF.Exp), [nbp], [Binv])
        mk.op("act", lambda e: e.activation(out=Dec.a, in_=nbl3, func=AF.Exp, scale=-1.0), [nblp], [Dec])
        for i_ in range(2):
            mk.op("dve", lambda e, i_=i_: e.tensor_tensor(out=ltmp.a, in0=lamb.a[:, i_ * 128:i_ * 128 + 64],
                                                          in1=lamb.a[:, i_ * 128 + 64:i_ * 128 + 128], op=ALU.mult),
                  [lamb], [ltmp])
            mk.op("dve", lambda e, i_=i_: e.tensor_reduce(out=ls.a[:, i_:i_ + 1], in_=ltmp.a, axis=AX.X, op=ALU.add),
                  [ltmp], [ls])
        mk.op("act", lambda e: e.activation(out=ls.a, in_=ls.a, func=AF.Exp), [ls], [ls])
        mk.op("dve", lambda e: e.tensor_tensor(out=neglam.a, in0=ls.a[:, 1:2], in1=ls.a[:, 0:1], op=ALU.subtract), [ls], [neglam])
        mk.op("dve", lambda e: e.tensor_scalar(out=neglam.a, in0=neglam.a, scalar1=-float(lam_inits[l]), scalar2=None,
                                               op0=ALU.add), [neglam], [neglam])
        mk.barrier()

        m2 = A.top
        mq = [A.tile([4, 128], BF16) for _ in range(2)]
        mkk = [A.tile([4, 128], BF16) for _ in range(2)]
        mkn = [A.tile([512], BF16) for _ in range(2)]
        Vm = [A.tile([8, 129], BF16) for _ in range(2)]
        ogt = [A.tile([1024], BF16) for _ in range(2)]
        hbS = [A.tile([1024], BF16) for _ in range(2)]
        C32 = [A.tile([129], F32) for _ in range(8)]
        Cb = [A.tile([129], BF16) for _ in range(8)]
        ATm = [A.tile([128], BF16) for _ in range(3)]
        Vk = [A.tile([129], BF16) for _ in range(3)]
        dd = [A.tile([1], F32) for _ in range(3)]
        ssh = [A.tile([1], F32) for _ in range(3)]
        hbf = [A.tile([128], F32) for _ in range(3)]
        junk2 = A.tile([128], BF16)
        ATp = [psb(i * 128, 128) for i in range(4)]
        BRp = [psb(512 + i * 256, 129) for i in range(4)]
        dCp = [psb(1536 + i * 256, 129) for i in range(4)]
        for hh in range(8):
            mk.op("pool", lambda e, hh=hh: e.memset(C32[hh].a, 0.0), [], [C32[hh]])
            mk.op("pool", lambda e, hh=hh: e.memset(Cb[hh].a, 0.0), [], [Cb[hh]])
        for b_ in range(2):
            mk.op("pool", lambda e, b_=b_: e.memset(Vm[b_].a[:, :, 128:129], 1.0), [], [Vm[b_]])
        n = 0
        for c in range(NT):
            cs_ = slice(c * 128, (c + 1) * 128)
            q_, k_, kn_, V_, og_, hS = mq[c % 2], mkk[c % 2], mkn[c % 2], Vm[c % 2], ogt[c % 2], hbS[c % 2]
            load(q_.a, mqT_scr[:, :, cs_].rearrange("a p t -> p a t"), [R["mqT"]], [q_])
            load(k_.a, mkT_scr[:, :, cs_].rearrange("a p t -> p a t"), [R["mkT"]], [k_])
            load(kn_.a, mk_scr[cs_, :], [R["mk"]], [kn_])
            load(V_.a[:, :, 0:128], mv_scr[cs_, :].rearrange("p (h d) -> p h d", h=8), [R["mv"]], [V_], waw=False)
            load(og_.a, og_scr[cs_, :], [R["og"]], [og_])
            for hd in range(8):
                pr = hd // 2
                rows = slice((hd % 2) * 64, (hd % 2) * 64 + 64)
                AT, BR, dC = ATp[n % 4], BRp[n % 4], dCp[n % 4]
                am, vk, dd_, ssh_, hb_ = ATm[n % 3], Vk[n % 3], dd[n % 3], ssh[n % 3], hbf[n % 3]
                n += 1
                C3, Cb_ = C32[hd], Cb[hd]
                mk.op("pe", lambda e, AT=AT, k_=k_, q_=q_, rows=rows, pr=pr: e.matmul(
                    AT.a, lhsT=k_.a[rows, pr, :], rhs=q_.a[rows, pr, :], start=True, stop=True), [k_, q_], [AT])
                mk.op("dve", lambda e, AT=AT, am=am, c=c, hd=hd: e.scalar_tensor_tensor(
                    out=am.a, in0=AT.a, scalar=Aal.a[:, c, hd:hd + 1], in1=tri_b.a, op0=ALU.mult, op1=ALU.mult),
                    [AT, Aal, tri_b], [am])
                mk.op("pool", lambda e, vk=vk, V_=V_, c=c, hd=hd: e.tensor_scalar(
                    out=vk.a, in0=V_.a[:, hd, :], scalar1=Kp.a[:, c, hd:hd + 1], scalar2=None, op0=ALU.mult),
                    [V_, Kp], [vk])
                mk.op("pe", lambda e, BR=BR, am=am, V_=V_, hd=hd: e.matmul(
                    BR.a, lhsT=am.a, rhs=V_.a[:, hd, :], start=True, stop=False), [am, V_], [BR], signal=False)
                mk.op("pe", lambda e, BR=BR, q_=q_, Cb_=Cb_, rows=rows, pr=pr: e.matmul(
                    BR.a, lhsT=q_.a[rows, pr, :], rhs=Cb_.a[rows, :], start=False, stop=True), [q_, Cb_], [BR], waw=False)
                mk.op("pe", lambda e, dC=dC, kn_=kn_, vk=vk, pr=pr: e.matmul(
                    dC.a, lhsT=kn_.a[:, pr * 128:(pr + 1) * 128], rhs=vk.a, start=True, stop=True), [kn_, vk], [dC])
                mk.op("dve", lambda e, C3=C3, dC=dC, rows=rows, c=c, hd=hd: e.scalar_tensor_tensor(
                    out=C3.a[rows, :], in0=C3.a[rows, :], scalar=Dec.a[rows, c, hd:hd + 1], in1=dC.a[rows, :],
                    op0=ALU.mult, op1=ALU.add), [C3, Dec, dC], [C3])
                mk.op("act", lambda e, C3=C3, Cb_=Cb_, rows=rows: e.copy(out=Cb_.a[rows, :], in_=C3.a[rows, :]), [C3], [Cb_])
                mk.op("act", lambda e, BR=BR, dd_=dd_: e.activation(out=dd_.a, in_=BR.a[:, 128:129], func=AF.Abs), [BR], [dd_])
                mk.op("dve", lambda e, dd_=dd_, c=c, hd=hd: e.tensor_tensor(
                    out=dd_.a, in0=dd_.a, in1=Binv.a[:, c, hd:hd + 1], op=ALU.max), [dd_, Binv], [dd_])
                mk.op("dve", lambda e, dd_=dd_: e.reciprocal(out=dd_.a, in_=dd_.a), [dd_], [dd_])
                mk.op("dve", lambda e, BR=BR, dd_=dd_, hb_=hb_: e.tensor_scalar(
                    out=hb_.a, in0=BR.a[:, 0:128], scalar1=dd_.a[:, 0:1], scalar2=None, op0=ALU.mult), [BR, dd_], [hb_])
                mk.op("act", lambda e, hb_=hb_, ssh_=ssh_: e.activation(out=junk2.a, in_=hb_.a, func=AF.Square,
                                                                      accum_out=ssh_.a), [hb_], [junk2, ssh_])
                rsqrt_act(mk, ssh_, ssh_, 128)
                mk.op("dve", lambda e, hb_=hb_, ssh_=ssh_, hS=hS, og_=og_, hd=hd: e.scalar_tensor_tensor(
                    out=hS.a[:, hd * 128:(hd + 1) * 128], in0=hb_.a, scalar=ssh_.a[:, 0:1],
                    in1=og_.a[:, hd * 128:(hd + 1) * 128], op0=ALU.mult, op1=ALU.mult), [hb_, ssh_, og_], [hS], waw=False)
            store(hb_scr[cs_, :], hS.a, [hS], [R["hb"]])
        mk.barrier()
        A.top = m2

        qTb = [A.tile([S], BF16) for _ in range(2)]
        kTb = [A.tile([S], BF16) for _ in range(2)]
        Vb = [A.tile([NT, 129], BF16) for _ in range(2)]
        zsb = [A.tile([4, 128], BF16) for _ in range(2)]
        Eb = [A.tile([2, 512], BF16) for _ in range(3)]
        accS = [[A.tile([4, 129], F32) for _ in range(2)] for _ in range(2)]
        oaS = [A.tile([4, 128], BF16) for _ in range(2)]
        rc = [A.tile([2], F32) for _ in range(2)]
        o0 = [A.tile([128], F32) for _ in range(2)]
        dt_ = [A.tile([128], F32) for _ in range(2)]
        ssd = [A.tile([1], F32) for _ in range(2)]
        STb = [psb(i * 1024, 1024) for i in range(2)]
        accs = [psb(2048 + r_ * 1024, 1024) for r_ in range(2)]
        for b_ in range(2):
            mk.op("pool", lambda e, b_=b_: e.memset(Vb[b_].a[:, :, 128:129], 1.0), [], [Vb[b_]])
        g = 0
        ep = 0
        for hd in range(8):
            qT, kT, Vg = qTb[hd % 2], kTb[hd % 2], Vb[hd % 2]
            load(qT.a, qT_scr[hd], [R["qT"]], [qT])
            load(kT.a, kT_scr[hd], [R["kT"]], [kT])
            load(Vg.a[:, :, 0:128], v_scr[:, hd * 128:(hd + 1) * 128].rearrange("(c p) d -> p c d", p=128), [R["v"]], [Vg],
                 waw=False)
            for Q in range(NQ):
                zs = zsb[(hd * NQ + Q) % 2]
                qsl = slice(Q * 512, (Q + 1) * 512)
                load(zs.a, zsa_scr[qsl, hd * 128:(hd + 1) * 128].rearrange("(j p) d -> p j d", p=128), [R["zsa"]], [zs])
                aS = accS[(hd * NQ + Q) % 2]
                for r_ in range(2):
                    rows = slice(r_ * 64, (r_ + 1) * 64)
                    acc = accs[r_]
                    nk = 4 * (Q + 1)
                    started = [False, False]
                    for G_ in range(nk // 2):
                        ST = STb[g % 2]
                        E = Eb[g % 3]
                        g += 1
                        for ii in range(2):
                            kt = 2 * G_ + ii
                            mk.op("pe", lambda e, ST=ST, kT=kT, qT=qT, rows=rows, kt=kt, ii=ii, qsl=qsl: e.matmul(
                                ST.a[:, ii * 512:(ii + 1) * 512], lhsT=kT.a[rows, kt * 128:(kt + 1) * 128], rhs=qT.a[rows, qsl],
                                start=True, stop=True), [kT, qT], [ST], waw=(ii == 0), signal=(ii == 1))
                        mk.op("act", lambda e, ST=ST, E=E: e.activation(out=E.a.rearrange("p a b -> p (a b)"), in_=ST.a,
                                                                       func=AF.Exp, scale=0.125), [ST], [E])
                        if 2 * G_ >= 4 * Q:
                            for ii in range(2):
                                ir = 2 * G_ + ii - 4 * Q
                                mk.op("pool", lambda e, E=E, ii=ii, ir=ir: e.tensor_tensor(
                                    out=E.a[:, ii, ir * 128:(ir + 1) * 128], in0=E.a[:, ii, ir * 128:(ir + 1) * 128],
                                    in1=tri_b.a, op=ALU.mult), [E, tri_b], [E])
                        for j in range(4):
                            for ii in range(2):
                                kt = 2 * G_ + ii
                                ir = kt - 4 * Q
                                if ir > j:
                                    continue
                                bk = j // 2
                                stt = not started[bk]
                                started[bk] = True
                                last = (ir == j)
                                mk.op("pe", lambda e, acc=acc, E=E, Vg=Vg, j=j, ii=ii, kt=kt, stt=stt, last=last: e.matmul(
                                    acc.a[:, j * 256:j * 256 + 129], lhsT=E.a[:, ii, j * 128:(j + 1) * 128], rhs=Vg.a[:, kt, :],
                                    start=stt, stop=last, skip_group_check=True), [E, Vg], [acc], waw=stt, signal=last)
                    a_ = aS[r_]
                    eng = "act" if r_ == 0 else "dve"
                    src = acc.a.rearrange("p (j c) -> p j c", j=4)[:, :, 0:129]
                    if eng == "act":
                        mk.op("act", lambda e, a_=a_, src=src: e.copy(out=a_.a, in_=src), [acc], [a_])
                    else:
                        mk.op("dve", lambda e, a_=a_, src=src: e.tensor_copy(out=a_.a, in_=src), [acc], [a_])
                oS = oaS[(hd * NQ + Q) % 2]
                for j in range(4):
                    rc_, o0_, d_, ssd_ = rc[ep % 2], o0[ep % 2], dt_[ep % 2], ssd[ep % 2]
                    ep += 1
                    a0, a1 = aS[0], aS[1]
                    mk.op("dve", lambda e, rc_=rc_, a0=a0, j=j: e.reciprocal(out=rc_.a[:, 0:1], in_=a0.a[:, j, 128:129]),
                          [a0], [rc_])
                    mk.op("dve", lambda e, rc_=rc_, a1=a1, j=j: e.reciprocal(out=rc_.a[:, 1:2], in_=a1.a[:, j, 128:129]),
                          [a1], [rc_], waw=False)
                    mk.op("dve", lambda e, rc_=rc_: e.tensor_tensor(out=rc_.a[:, 1:2], in0=rc_.a[:, 1:2], in1=neglam.a,
                                                                   op=ALU.mult), [rc_, neglam], [rc_])
                    mk.op("dve", lambda e, o0_=o0_, a0=a0, rc_=rc_, j=j: e.tensor_scalar(
                        out=o0_.a, in0=a0.a[:, j, 0:128], scalar1=rc_.a[:, 0:1], scalar2=None, op0=ALU.mult), [a0, rc_], [o0_])
                    mk.op("dve", lambda e, d_=d_, a1=a1, rc_=rc_, o0_=o0_, j=j: e.scalar_tensor_tensor(
                        out=d_.a, in0=a1.a[:, j, 0:128], scalar=rc_.a[:, 1:2], in1=o0_.a, op0=ALU.mult, op1=ALU.add),
                        [a1, rc_, o0_], [d_])
                    mk.op("pool", lambda e, d_=d_, o0_=o0_: e.tensor_tensor(out=o0_.a, in0=d_.a, in1=d_.a, op=ALU.mult),
                          [d_], [o0_])
                    mk.op("dve", lambda e, o0_=o0_, ssd_=ssd_: e.tensor_reduce(out=ssd_.a, in_=o0_.a, axis=AX.X, op=ALU.add),
                          [o0_], [ssd_])
                    rsqrt_act(mk, ssd_, ssd_, 128)
                    mk.op("dve", lambda e, d_=d_, ssd_=ssd_, oS=oS, zs=zs, j=j: e.scalar_tensor_tensor(
                        out=oS.a[:, j, :], in0=d_.a, scalar=ssd_.a[:, 0:1], in1=zs.a[:, j, :], op0=ALU.mult, op1=ALU.mult),
                        [d_, ssd_, zs], [oS], waw=False)
                store(oa_scr[qsl, hd * 128:(hd + 1) * 128].rearrange("(j p) d -> p j d", p=128), oS.a, [oS], [R["oa"]])
        mk.barrier()
        A.top = m

    def phase3(l):
        m = A.top
        xsrc, Rx = (x_in, R["x_in"]) if l == 0 else (x_mid, R["x_mid"])
        dst, Rd = (x_mid, R["x_mid"]) if l < L - 1 else (out_d, R["out"])
        Rwo = R["wobf%d" % l]
        W3 = [A.tile([8, 1024], BF16) for _ in range(3)]
        for mi in range(3):
            load(W3[mi].a, wobf[l][mi], [Rwo], [W3[mi]])
        oat = [A.tile([1024], BF16) for _ in range(2)]
        hbt = [A.tile([1024], BF16) for _ in range(2)]
        sga = [A.tile([1024], BF16) for _ in range(2)]
        sgb = [A.tile([1024], BF16) for _ in range(2)]
        xr = [A.tile([1024], F32) for _ in range(2)]
        oaT = [A.tile([8, 128], BF16) for _ in range(2)]
        hbT = [A.tile([8, 128], BF16) for _ in range(2)]
        uT = [A.tile([8, 128], BF16) for _ in range(2)]
        u1 = [A.tile([1024], F32) for _ in range(2)]
        u2 = [A.tile([1024], F32) for _ in range(2)]
        ub = [A.tile([1024], BF16) for _ in range(2)]
        ot = [A.tile([1024], F32) for _ in range(2)]
        Ya = psb(0, 1024)
        Yb = psb(1024, 1024)
        Yo = psb(2048, 1024)
        pT = [psb(3072 + i * 512, 512, BF16, [8, 128]) for i in range(2)]
        np_ = 0
        for t in range(NT):
            b = t % 2
            ts_ = slice(t * 128, (t + 1) * 128)
            load(oat[b].a, oa_scr[ts_, :], [R["oa"]], [oat[b]])
            load(hbt[b].a, hb_scr[ts_, :], [R["hb"]], [hbt[b]])
            load(sga[b].a, sga_scr[ts_, :], [R["sga"]], [sga[b]])
            load(sgb[b].a, sgb_scr[ts_, :], [R["sgb"]], [sgb[b]])
            load(xr[b].a, xsrc[ts_, :], [Rx], [xr[b]])
            for (src, dstT, eng) in ((oat[b], oaT[b], "act"), (hbt[b], hbT[b], "dve")):
                p_ = pT[np_ % 2]
                np_ += 1
                for kc in range(8):
                    mk.op("pe", lambda e, p_=p_, src=src, kc=kc: e.transpose(out=p_.a[:, kc, :],
                                                                          in_=src.a[:, kc * 128:(kc + 1) * 128],
                                                                          identity=ident_b.a), [src, ident_b], [p_],
                          waw=(kc == 0), signal=(kc == 7))
                if eng == "act":
                    mk.op("act", lambda e, p_=p_, dstT=dstT: e.copy(out=dstT.a, in_=p_.a), [p_], [dstT])
                else:
                    mk.op("dve", lambda e, p_=p_, dstT=dstT: e.tensor_copy(out=dstT.a, in_=p_.a), [p_], [dstT])
            for (Y, lT, W) in ((Ya, oaT[b], W3[0]), (Yb, hbT[b], W3[1])):
                for half in range(2):
                    for kc in range(8):
                        c = half * 8 + kc
                        mk.op("pe", lambda e, Y=Y, lT=lT, W=W, half=half, kc=kc: e.matmul(
                            Y.a[:, half * 512:(half + 1) * 512], lhsT=lT.a[:, kc, :], rhs=W.a[:, kc, half * 512:(half + 1) * 512],
                            start=(kc == 0), stop=(kc == 7)), [lT, W], [Y], waw=(c == 0), signal=(c == 15))
            mk.op("dve", lambda e, b=b: e.tensor_tensor(out=u1[b].a, in0=Ya.a, in1=sga[b].a, op=ALU.mult), [Ya, sga[b]], [u1[b]])
            mk.op("dve", lambda e, b=b: e.tensor_tensor(out=u2[b].a, in0=Yb.a, in1=sgb[b].a, op=ALU.mult), [Yb, sgb[b]], [u2[b]])
            mk.op("pool", lambda e, b=b: e.tensor_tensor(out=ub[b].a, in0=u1[b].a, in1=u2[b].a, op=ALU.add),
                  [u1[b], u2[b]], [ub[b]])
            p_ = pT[np_ % 2]
            np_ += 1
            for kc in range(8):
                mk.op("pe", lambda e, p_=p_, b=b, kc=kc: e.transpose(out=p_.a[:, kc, :], in_=ub[b].a[:, kc * 128:(kc + 1) * 128],
                                                                  identity=ident_b.a), [ub[b], ident_b], [p_],
                      waw=(kc == 0), signal=(kc == 7))
            mk.op("act", lambda e, p_=p_, b=b: e.copy(out=uT[b].a, in_=p_.a), [p_], [uT[b]])
            for half in range(2):
                for kc in range(8):
                    c = half * 8 + kc
                    mk.op("pe", lambda e, b=b, half=half, kc=kc: e.matmul(
                        Yo.a[:, half * 512:(half + 1) * 512], lhsT=uT[b].a[:, kc, :],
                        rhs=W3[2].a[:, kc, half * 512:(half + 1) * 512], start=(kc == 0), stop=(kc == 7)),
                        [uT[b], W3[2]], [Yo], waw=(c == 0), signal=(c == 15))
            mk.op("dve", lambda e, b=b: e.tensor_tensor(out=ot[b].a, in0=Yo.a, in1=xr[b].a, op=ALU.add), [Yo, xr[b]], [ot[b]])
            store(dst[ts_, :], ot[b].a, [ot[b]], [Rd])
        mk.barrier()
        A.top = m

    for l in range(L):
        phase0(l)
    for l in range(L):
        phase1(l)
        phase2(l)
        phase3(l)
    mk.emit()
    return nc, mk


def _lam_init(l):
    return 0.8 - 0.6 * math.exp(-0.3 * l)


def make_in_map(xb, posb, S, norm_g, w_in, q_norm_g, k_norm_g, lambda_qk, attn_norm_g, w_out_a, conv_w, conv_b, igate_b,
                fgate_b, mlstm_norm_g, w_out_b, w_o):
    L = w_in.shape[0]
    NT = S // 128
    perm = np.concatenate([np.arange(0, 4096), np.arange(5120, 6144), np.arange(6160, 7184), np.arange(7184, 8208),
                           np.arange(8208, 9232), np.arange(9232, 10256), np.arange(4096, 5120), np.arange(6144, 6160)])
    f32 = np.float32
    inv_freq = (10000.0 ** (-np.arange(0, 64, 2, dtype=np.float32) / 64)).astype(np.float32)
    freq = np.tile((inv_freq.astype(np.float64) / (2.0 * math.pi)).astype(f32)[None, :], (128, 1))
    return {
        "x": np.ascontiguousarray(xb, dtype=f32),
        "pos": np.ascontiguousarray(posb.reshape(NT, 128).T.astype(np.int32)),
        "c_ident": np.eye(128, dtype=f32),
        "c_tri": np.triu(np.ones((128, 128), dtype=f32)),
        "c_freq": np.ascontiguousarray(freq),
        "w_in": np.ascontiguousarray(w_in[:, :, perm], dtype=f32),
        "w_out": np.ascontiguousarray(np.stack([w_out_a, w_out_b, w_o], axis=1), dtype=f32),
        "norm_g": np.ascontiguousarray(norm_g.reshape(L, 8, 128).transpose(0, 2, 1), dtype=f32),
        "q_g": np.ascontiguousarray(q_norm_g, dtype=f32),
        "k_g": np.ascontiguousarray(k_norm_g, dtype=f32),
        "lam_qk": np.ascontiguousarray(lambda_qk.reshape(L, 256), dtype=f32),
        "attn_g": np.ascontiguousarray(attn_norm_g.reshape(L, 128, 1), dtype=f32),
        "ml_g": np.ascontiguousarray(mlstm_norm_g.reshape(L, 128, 1), dtype=f32),
        "conv_w": np.ascontiguousarray(conv_w, dtype=f32),
        "conv_b": np.ascontiguousarray(conv_b, dtype=f32),
        "gate_b": np.ascontiguousarray(np.concatenate([igate_b, fgate_b], axis=1), dtype=f32),
    }


def kernel(x, positions, norm_g, w_in, q_norm_g, k_norm_g, lambda_qk, attn_norm_g, w_out_a, conv_w, conv_b, igate_b, fgate_b,
           mlstm_norm_g, w_out_b, w_o):
    x = np.asarray(x)
    positions = np.asarray(positions)
    Bsz, S, _ = x.shape
    L = int(np.asarray(w_in).shape[0])
    args = [np.asarray(a) for a in (norm_g, w_in, q_norm_g, k_norm_g, lambda_qk, attn_norm_g, w_out_a, conv_w, conv_b,
                                    igate_b, fgate_b, mlstm_norm_g, w_out_b, w_o)]
    nc, _ = build(S, L, [_lam_init(l) for l in range(L)])
    n = 8
    in_maps = [make_in_map(x[c % Bsz], positions[c % Bsz], S, *args) for c in range(n)]
    res = run_bass_kernel_spmd(nc, in_maps, core_ids=list(range(n)))
    out = np.stack([np.asarray(res.results[b]["out"], dtype=np.float32) for b in range(Bsz)], axis=0)
    return out
```

```python
import math
from contextlib import ExitStack

import numpy as np
import concourse.bass as bass
import concourse.mybir as mybir
from concourse.bass_utils import run_bass_kernel_spmd

F32 = mybir.dt.float32
BF16 = mybir.dt.bfloat16
I32 = mybir.dt.int32
AF = mybir.ActivationFunctionType
ALU = mybir.AluOpType
AX = mybir.AxisListType

D = 1024
NH = 8
EPS = 1e-6
DIN = 10256
SELF_WAIT = True
PIPE_ML = False
DTSZ = {F32: 4, BF16: 2, I32: 4}


class Res:
    __slots__ = ("w", "r")

    def __init__(self):
        self.w = {}
        self.r = {}


class Buf(Res):
    __slots__ = ("a",)

    def __init__(self, ap):
        super().__init__()
        self.a = ap


class Chan:
    def __init__(self, sem, key):
        self.sem = sem
        self.key = key
        self.cnt = 0


class MK:
    ENG = ("pe", "act", "dve", "pool", "sp")

    def __init__(self, nc, es):
        self.nc = nc
        self.es = es
        self.q = {e: [] for e in self.ENG}
        self.sem = {e: es.enter_context(nc.semaphore("sem_" + e)) for e in self.ENG}
        self.cnt = {e: 0 for e in self.ENG}
        self.seen = {e: {} for e in self.ENG}
        self.chans = []
        self.ninst = 0

    def chan(self):
        k = "ch%d" % len(self.chans)
        c = Chan(self.es.enter_context(self.nc.semaphore(k)), k)
        self.chans.append(c)
        return c

    def _collect(self, eng, reads, writes, waw):
        need = {}
        seen = self.seen[eng]

        def add(d):
            for k, (s, v, src) in d.items():
                if src == eng and (eng == "pe" or not SELF_WAIT):
                    continue
                if seen.get(k, 0) >= v:
                    continue
                if k not in need or need[k][1] < v:
                    need[k] = (s, v)

        for r in reads:
            add(r.w)
        for w in writes:
            add(w.r)
            if waw:
                add(w.w)
        for k, (s, v) in need.items():
            seen[k] = v
        return list(need.values())

    def _mark(self, key, tok, reads, writes, waw):
        for r in reads:
            r.r[key] = tok
        for w in writes:
            if waw:
                w.w = {key: tok}
            else:
                w.w[key] = tok
            w.r = {}

    def op(self, eng, fn, reads=(), writes=(), waw=True, signal=True):
        toks = self._collect(eng, reads, writes, waw)
        sem = self.sem[eng]
        if signal:
            self.cnt[eng] += 1
            val = self.cnt[eng]
        else:
            val = self.cnt[eng] + 1
        self.q[eng].append((toks, fn, sem if signal else None, 1))
        self._mark(eng, (sem, val, eng), reads, writes, waw)
        self.ninst += 1

    def dma(self, q, ch, out, in_, reads=(), writes=(), waw=True):
        toks = self._collect(q, reads, writes, waw)
        ch.cnt += 16
        self.q[q].append((toks, lambda e: e.dma_start(out=out, in_=in_), ch.sem, 16))
        self._mark(ch.key, (ch.sem, ch.cnt, "dma"), reads, writes, waw)
        self.ninst += 1

    def barrier(self):
        allt = [(self.sem[e], self.cnt[e], e) for e in self.ENG if self.cnt[e] > 0]
        allt += [(c.sem, c.cnt, c.key) for c in self.chans if c.cnt > 0]
        for eng in self.ENG:
            toks = []
            for (s, v, k) in allt:
                if k == eng:
                    continue
                if self.seen[eng].get(k, 0) >= v:
                    continue
                self.seen[eng][k] = v
                toks.append((s, v))
            if toks:
                self.q[eng].append((toks, None, None, 0))

    def emit(self):
        nc = self.nc
        q = self.q

        def run(e, lst):
            for toks, fn, sem, inc in lst:
                for (s, v) in toks:
                    e.wait_ge(s, v)
                if fn is None:
                    continue
                ins = fn(e)
                if sem is not None:
                    ins.then_inc(sem, inc)

        with nc.Block() as block:
            @block.sync
            def _(e):
                run(e, q["sp"])

            @block.tensor
            def _(e):
                run(e, q["pe"])

            @block.scalar
            def _(e):
                run(e, q["act"])

            @block.vector
            def _(e):
                run(e, q["dve"])

            @block.gpsimd
            def _(e):
                run(e, q["pool"])


class Arena:
    def __init__(self, ap, words):
        self.ap = ap
        self.words = words
        self.top = 0

    def tile(self, free, dt):
        n = int(np.prod(free))
        w = (n * DTSZ[dt] + 3) // 4
        w = (w + 7) // 8 * 8
        assert self.top + w <= self.words, ("arena overflow", self.top, w, self.words)
        v = self.ap[:, self.top:self.top + w]
        self.top += w
        if dt != F32:
            v = v.bitcast(dt)
        v = v[:, 0:n]
        if len(free) == 2:
            v = v.rearrange("p (a b) -> p a b", a=free[0])
        elif len(free) == 3:
            v = v.rearrange("p (a b c) -> p a b c", a=free[0], b=free[1])
        return Buf(v)


def rsqrt_act(mk, src, dst, n):
    mk.op("act", lambda e: e.activation(out=dst.a, in_=src.a, func=AF.Ln, scale=1.0 / n, bias=EPS), [src], [dst])
    mk.op("act", lambda e: e.activation(out=dst.a, in_=dst.a, func=AF.Exp, scale=-0.5), [dst], [dst])


def bc(ap, axis, shape):
    return ap.unsqueeze(axis).to_broadcast(list(shape))


def build(S, L=2, lam_inits=None, debug=False):
    NT = S // 128
    TS = min(1024, S)
    NST = S // TS
    NTS = TS // 128
    NQ = S // 512
    nc = bass.Bass("TRN2", target_bir_lowering=False)
    es = ExitStack()

    def din(name, shape, dt=F32):
        return nc.dram_tensor(name, list(shape), dt, kind="ExternalInput").ap()

    def dscr(name, shape, dt):
        return nc.dram_tensor(name, list(shape), dt, kind="Internal").ap()

    x_in = din("x", [S, D])
    pos_in = din("pos", [128, NT], I32)
    cst_ident = din("c_ident", [128, 128])
    cst_tri = din("c_tri", [128, 128])
    cst_freq = din("c_freq", [128, 32])
    w_in = din("w_in", [L, D, DIN])
    w_out = din("w_out", [L, 3, D, D])
    norm_g = din("norm_g", [L, 128, 8])
    q_g = din("q_g", [L, 64])
    k_g = din("k_g", [L, 64])
    lam_qk = din("lam_qk", [L, 256])
    attn_g = din("attn_g", [L, 128, 1])
    ml_g = din("ml_g", [L, 128, 1])
    conv_w = din("conv_w", [L, 4, 1024])
    conv_b = din("conv_b", [L, 1024])
    gate_b = din("gate_b", [L, 16])
    out_d = nc.dram_tensor("out", [S, D], F32, kind="ExternalOutput").ap()

    wbf = [dscr("wbf%d" % l, [27, 128, 8, 512], BF16) for l in range(L)]
    wobf = [dscr("wobf%d" % l, [3, 128, 8, 1024], BF16) for l in range(L)]
    qT_scr = dscr("qT_scr", [8, 128, S], BF16)
    kT_scr = dscr("kT_scr", [8, 128, S], BF16)
    v_scr = dscr("v_scr", [S, D], BF16)
    zsa_scr = dscr("zsa_scr", [S, D], BF16)
    mqT_scr = dscr("mqT_scr", [4, 128, S], BF16)
    mkT_scr = dscr("mkT_scr", [4, 128, S], BF16)
    mk_scr = dscr("mk_scr", [S, 512], BF16)
    mv_scr = dscr("mv_scr", [S, D], BF16)
    og_scr = dscr("og_scr", [S, D], BF16)
    sga_scr = dscr("sga_scr", [S, D], BF16)
    sgb_scr = dscr("sgb_scr", [S, D], BF16)
    oa_scr = dscr("oa_scr", [S, D], BF16)
    hb_scr = dscr("hb_scr", [S, D], BF16)
    x_mid = dscr("x_mid", [S, D], F32)
    R = {n: Res() for n in ["x_in", "wbf0", "wbf1", "wobf0", "wobf1", "qT", "kT", "v", "zsa", "mqT", "mkT", "mk", "mv",
                            "og", "sga", "sgb", "oa", "hb", "x_mid", "out", "const"]}

    AW = 53000
    arena_t = es.enter_context(nc.sbuf_tensor("arena", [128, AW], F32))
    ps_t = es.enter_context(nc.psum_tensor("ps", [128, 4096], F32))
    mk = MK(nc, es)
    A = Arena(arena_t, AW)

    def psb(c0, ncol, dt=F32, free=None):
        v = ps_t[:, c0:c0 + ncol]
        if dt != F32:
            v = v.bitcast(dt)
        if free is not None:
            if len(free) == 2:
                v = v.rearrange("p (a b) -> p a b", a=free[0])
        return Buf(v)

    ch_ld = [mk.chan() for _ in range(6)]
    ch_st = [mk.chan() for _ in range(6)]
    ldi = [0]
    sti = [0]

    def load(out, in_, reads, writes, waw=True, ch=None):
        if ch is None:
            ch = ch_ld[ldi[0] % len(ch_ld)]
            ldi[0] += 1
        mk.dma("sp", ch, out, in_, reads=reads, writes=writes, waw=waw)

    def store(out, in_, reads, writes, waw=False, ch=None):
        if ch is None:
            ch = ch_st[sti[0] % len(ch_st)]
            sti[0] += 1
        mk.dma("pool", ch, out, in_, reads=reads, writes=writes, waw=waw)

    ident_f = A.tile([128], F32)
    ident_b = A.tile([128], BF16)
    tri_f = A.tile([128], F32)
    tri_b = A.tile([128], BF16)
    ones_f = A.tile([128], F32)
    mhalf = A.tile([16], F32)
    cosT = A.tile([NT, 32], F32)
    sinT = A.tile([NT, 32], F32)
    Graw = A.tile([NT, 16], F32)
    PERS = A.top

    load(ident_f.a, cst_ident, [R["const"]], [ident_f])
    load(tri_f.a, cst_tri, [R["const"]], [tri_f])
    mk.op("dve", lambda e: e.tensor_copy(out=ident_b.a, in_=ident_f.a), [ident_f], [ident_b])
    mk.op("dve", lambda e: e.tensor_copy(out=tri_b.a, in_=tri_f.a), [tri_f], [tri_b])
    mk.op("pool", lambda e: e.memset(ones_f.a, 1.0), [], [ones_f])
    mk.op("pool", lambda e: e.memset(mhalf.a, -0.5), [], [mhalf])

    m0 = A.top
    freq = A.tile([32], F32)
    posi = A.tile([NT], I32)
    posf = A.tile([NT], F32)
    u = A.tile([NT, 32], F32)
    ui = A.tile([NT, 32], I32)
    uf = A.tile([NT, 32], F32)
    load(freq.a, cst_freq, [R["const"]], [freq])
    load(posi.a, pos_in, [R["const"]], [posi])
    mk.op("dve", lambda e: e.tensor_copy(out=posf.a, in_=posi.a), [posi], [posf])
    mk.op("dve", lambda e: e.tensor_tensor(out=u.a, in0=bc(posf.a, 2, [128, NT, 32]), in1=bc(freq.a, 1, [128, NT, 32]),
                                           op=ALU.mult), [posf, freq], [u])
    for tab, off in ((sinT, 0.0), (cosT, 0.25)):
        if off != 0.0:
            mk.op("dve", lambda e: e.tensor_scalar(out=u.a, in0=u.a, scalar1=0.25, scalar2=None, op0=ALU.add), [u], [u])
        mk.op("dve", lambda e: e.tensor_copy(out=ui.a, in_=u.a), [u], [ui])
        mk.op("dve", lambda e: e.tensor_copy(out=uf.a, in_=ui.a), [ui], [uf])
        mk.op("dve", lambda e: e.tensor_tensor(out=uf.a, in0=u.a, in1=uf.a, op=ALU.subtract), [u, uf], [uf])
        mk.op("dve", lambda e: e.tensor_scalar(out=uf.a, in0=uf.a, scalar1=-0.49999, scalar2=0.49999, op0=ALU.max,
                                               op1=ALU.min), [uf], [uf])
        mk.op("act", lambda e, tab=tab: e.activation(out=tab.a, in_=uf.a, func=AF.Sin, scale=2.0 * math.pi), [uf], [tab])
    mk.barrier()
    A.top = m0

    def phase0(l):
        m = A.top
        gcol = A.tile([8], F32)
        gA = A.tile([1], F32)
        gB = A.tile([1], F32)
        cwb = A.tile([4, 1024], F32)
        wt = [A.tile([2048], F32) for _ in range(2)]
        wb = [A.tile([2048], BF16) for _ in range(2)]
        wb4 = [A.tile([1024], BF16) for _ in range(2)]
        Rw, Rwo = R["wbf%d" % l], R["wobf%d" % l]
        load(gcol.a, norm_g[l], [R["const"]], [gcol])
        load(gA.a, attn_g[l], [R["const"]], [gA])
        load(gB.a, ml_g[l], [R["const"]], [gB])
        load(cwb.a, conv_w[l].rearrange("j c -> (j c)").partition_broadcast(128).rearrange("p (j c) -> p j c", j=4),
             [R["const"]], [cwb])
        mk.op("dve", lambda e: e.tensor_scalar(out=gA.a, in0=gA.a, scalar1=float(1.0 - lam_inits[l]), scalar2=None,
                                               op0=ALU.mult), [gA], [gA])
        n = 0
        for kc in range(8):
            rows = slice(kc * 128, (kc + 1) * 128)
            gs = gcol.a[:, kc:kc + 1]
            for ci in range(5):
                t, b = wt[n % 2], wb[n % 2]
                load(t.a, w_in[l, rows, ci * 2048:(ci + 1) * 2048], [R["const"]], [t])
                if ci < 4:
                    if n % 2 == 0:
                        mk.op("dve", lambda e, t=t, b=b, gs=gs: e.tensor_scalar(out=b.a, in0=t.a, scalar1=gs, scalar2=None,
                                                                             op0=ALU.mult), [t, gcol], [b])
                    else:
                        mk.op("act", lambda e, t=t, b=b, gs=gs: e.activation(out=b.a, in_=t.a, func=AF.Copy, scale=gs),
                              [t, gcol], [b])
                    store(wbf[l][ci * 4:ci * 4 + 4, :, kc, :].rearrange("b p c -> p b c"),
                          b.a.rearrange("p (b c) -> p b c", b=4), [b], [Rw])
                else:
                    mk.op("act", lambda e, t=t, b=b, gs=gs: e.activation(out=b.a[:, 0:1024], in_=t.a[:, 0:1024], func=AF.Copy,
                                                                       scale=gs), [t, gcol], [b])
                    store(wbf[l][16:18, :, kc, :].rearrange("b p c -> p b c"),
                          b.a[:, 0:1024].rearrange("p (b c) -> p b c", b=2), [b], [Rw])
                    mk.op("dve", lambda e, t=t, gs=gs: e.tensor_scalar(out=t.a[:, 1024:2048], in0=t.a[:, 1024:2048],
                                                                   scalar1=gs, scalar2=None, op0=ALU.mult), [t, gcol], [t])
                    for j in range(4):
                        b4 = wb4[j % 2]
                        eng = "dve" if j % 2 == 0 else "pool"
                        mk.op(eng, lambda e, t=t, b4=b4, j=j: e.tensor_tensor(out=b4.a, in0=t.a[:, 1024:2048],
                                                                             in1=cwb.a[:, j, :], op=ALU.mult), [t, cwb], [b4])
                        store(wbf[l][18 + j, :, kc, :], b4.a[:, 0:512], [b4], [Rw])
                        store(wbf[l][22 + j, :, kc, :], b4.a[:, 512:1024], [b4], [Rw])
                n += 1
            t, b = wt[n % 2], wb[n % 2]
            load(t.a[:, 0:16], w_in[l, rows, 10240:10256], [R["const"]], [t])
            mk.op("dve", lambda e, t=t, b=b, gs=gs: e.tensor_scalar(out=b.a[:, 0:16], in0=t.a[:, 0:16], scalar1=gs,
                                                                 scalar2=None, op0=ALU.mult), [t, gcol], [b])
            store(wbf[l][26, :, kc, 0:16], b.a[:, 0:16], [b], [Rw])
            n += 1
            for mi in range(3):
                t, b = wt[n % 2], wb[n % 2]
                load(t.a[:, 0:1024], w_out[l, mi, rows, :], [R["const"]], [t])
                if mi == 2:
                    mk.op("act", lambda e, t=t, b=b: e.copy(out=b.a[:, 0:1024], in_=t.a[:, 0:1024]), [t], [b])
                else:
                    gg = gA if mi == 0 else gB
                    mk.op("dve", lambda e, t=t, b=b, gg=gg: e.tensor_scalar(out=b.a[:, 0:1024], in0=t.a[:, 0:1024],
                                                                         scalar1=gg.a[:, 0:1], scalar2=None, op0=ALU.mult),
                          [t, gg], [b])
                store(wobf[l][mi, :, kc, :], b.a[:, 0:1024], [b], [Rwo])
                n += 1
        mk.barrier()
        A.top = m

    def phase1(l):
        m = A.top
        xsrc, Rx = (x_in, R["x_in"]) if l == 0 else (x_mid, R["x_mid"])
        Rw = R["wbf%d" % l]
        xt = [A.tile([1024], F32) for _ in range(2)]
        xb = [A.tile([1024], BF16) for _ in range(2)]
        junk = A.tile([1024], BF16)
        ss = [A.tile([1], F32) for _ in range(2)]
        rstd = [A.tile([1], F32) for _ in range(2)]
        hT = [A.tile([8, 3 + TS], BF16) for _ in range(2)]
        NWB = 6
        wblk = [A.tile([8, 512], BF16) for _ in range(NWB)]
        stT = [A.tile([4, TS], BF16) for _ in range(2)]
        stN = [A.tile([NTS, 512], BF16) for _ in range(2)]
        sbo = A.tile([NTS, 512], BF16)
        ta = [A.tile([512], F32) for _ in range(3)]
        tb2 = [A.tile([512], F32) for _ in range(3)]
        qn = [A.tile([8, 64], F32) for _ in range(3)]
        t1 = [A.tile([8, 32], F32) for _ in range(3)]
        t2 = [A.tile([8, 32], F32) for _ in range(3)]
        t3 = [A.tile([8, 32], F32) for _ in range(3)]
        t4 = [A.tile([8, 32], F32) for _ in range(3)]
        qrot = [A.tile([8, 64], BF16) for _ in range(3)]
        ybf = [A.tile([512], BF16) for _ in range(3)]
        s8 = [A.tile([8], F32) for _ in range(3)]
        RT = {"q": A.tile([NTS, 4, 32], F32), "k": A.tile([NTS, 4, 32], F32)}
        gq = {"q": A.tile([64], F32), "k": A.tile([64], F32)}
        cbb = A.tile([1024], F32)
        load(gq["q"].a, q_g[l].partition_broadcast(128), [R["const"]], [gq["q"]])
        load(gq["k"].a, k_g[l].partition_broadcast(128), [R["const"]], [gq["k"]])
        load(cbb.a, conv_b[l].partition_broadcast(128), [R["const"]], [cbb])
        Pm = [psb(i * 512, 512) for i in range(4)]
        psX = [psb(2048 + i * 512, 512, BF16, [8, 128]) for i in range(2)]
        psQ = [psb(3072 + i * 256, 256, BF16, [4, 128]) for i in range(2)]
        cnt = {"p": 0, "w": 0, "e": 0, "q": 0, "stT": 0, "stN": 0}

        blocks = [("qk", "q", 0, 0), ("qk", "q", 1, 1), ("qk", "k", 0, 2), ("qk", "k", 1, 3),
                  ("copy", "v", 0, 4), ("copy", "v", 1, 5), ("silu", "zsa", 0, 6), ("silu", "zsa", 1, 7),
                  ("copy", "mv", 0, 8), ("copy", "mv", 1, 9),
                  ("sigk", None, 0, 10), ("silum", "og", 0, 12), ("sigk", None, 1, 11), ("silum", "og", 1, 13),
                  ("sig", "sga", 0, 14), ("sig", "sga", 1, 15), ("sig", "sgb", 0, 16), ("sig", "sgb", 1, 17),
                  ("conv", "q", 0, 18), ("conv", "k", 0, 22), ("gates", None, 0, 26)]
        scr = {"v": v_scr, "zsa": zsa_scr, "mv": mv_scr, "og": og_scr, "sga": sga_scr, "sgb": sgb_scr}

        for st in range(NST):
            h = hT[st % 2]
            tok0 = st * TS
            if st == 0:
                mk.op("pool", lambda e, h=h: e.memset(h.a[:, :, 0:3], 0.0), [], [h])
            else:
                hp = hT[(st - 1) % 2]
                mk.op("pool", lambda e, h=h, hp=hp: e.tensor_copy(out=h.a[:, :, 0:3], in_=hp.a[:, :, TS:TS + 3]), [hp], [h])
            for i in range(NTS):
                x_, xb_, ss_, rs_ = xt[i % 2], xb[i % 2], ss[i % 2], rstd[i % 2]
                pX = psX[i % 2]
                load(x_.a, xsrc[tok0 + i * 128: tok0 + (i + 1) * 128, :], [Rx], [x_])
                mk.op("act", lambda e, x_=x_, ss_=ss_: e.activation(out=junk.a, in_=x_.a, func=AF.Square, accum_out=ss_.a),
                      [x_], [junk, ss_])
                rsqrt_act(mk, ss_, rs_, D)
                mk.op("dve", lambda e, x_=x_, xb_=xb_, rs_=rs_: e.tensor_scalar(out=xb_.a, in0=x_.a, scalar1=rs_.a[:, 0:1],
                                                                             scalar2=None, op0=ALU.mult), [x_, rs_], [xb_])
                for kc in range(8):
                    mk.op("pe", lambda e, pX=pX, xb_=xb_, kc=kc: e.transpose(out=pX.a[:, kc, :],
                                                                           in_=xb_.a[:, kc * 128:(kc + 1) * 128],
                                                                           identity=ident_b.a),
                          [xb_, ident_b], [pX], waw=(kc == 0), signal=(kc == 7))
                mk.op("act", lambda e, h=h, pX=pX, i=i: e.copy(out=h.a[:, :, 3 + i * 128: 3 + (i + 1) * 128], in_=pX.a),
                      [pX], [h], waw=False)
            cs = cosT.a[:, st * NTS:(st + 1) * NTS, :]
            sn = sinT.a[:, st * NTS:(st + 1) * NTS, :]
            for wh in ("q", "k"):
                g1 = bc(gq[wh].a[:, 0:32], 1, [128, NTS, 32])
                g2 = bc(gq[wh].a[:, 32:64], 1, [128, NTS, 32])
                rt = RT[wh]
                for k_, (a_, g_) in enumerate(((cs, g1), (sn, g2), (cs, g2), (sn, g1))):
                    mk.op("pool", lambda e, rt=rt, k_=k_, a_=a_, g_=g_: e.tensor_tensor(out=rt.a[:, :, k_, :], in0=a_, in1=g_,
                                                                                   op=ALU.mult),
                          [cosT, sinT, gq[wh]], [rt], waw=(k_ == 0))

            for (role, name, half, bidx) in blocks:
                nsub = 4 if role == "conv" else 1
                ws = []
                for j in range(nsub):
                    w = wblk[cnt["w"] % NWB]
                    cnt["w"] += 1
                    if role == "gates":
                        load(w.a[:, :, 0:16], wbf[l][bidx, :, :, 0:16], [Rw], [w])
                    else:
                        load(w.a, wbf[l][bidx + j], [Rw], [w])
                    ws.append(w)
                ncol = 16 if role == "gates" else 512
                sT = sN = None
                if role in ("qk", "conv"):
                    sT = stT[cnt["stT"] % 2]
                    cnt["stT"] += 1
                if role in ("copy", "silu", "silum", "sig") or (role == "conv" and name == "k"):
                    sN = stN[cnt["stN"] % 2]
                    cnt["stN"] += 1
                for i in range(NTS):
                    P = Pm[cnt["p"] % 4]
                    cnt["p"] += 1
                    nmm = nsub * 8
                    c = 0
                    for j in range(nsub):
                        for kc in range(8):
                            off = (j if role == "conv" else 3) + i * 128
                            mk.op("pe", lambda e, P=P, h=h, w=ws[j], kc=kc, off=off, c=c, nmm=nmm, ncol=ncol: e.matmul(
                                P.a[:, 0:ncol], lhsT=h.a[:, kc, off:off + 128], rhs=w.a[:, kc, 0:ncol],
                                start=(c == 0), stop=(c == nmm - 1)), [h, ws[j]], [P], waw=(c == 0), signal=(c == nmm - 1))
                            c += 1
                    e_ = cnt["e"] % 3
                    cnt["e"] += 1
                    if role == "copy":
                        eng = "act" if i % 2 == 0 else "dve"
                        if eng == "act":
                            mk.op("act", lambda e, P=P, sN=sN, i=i: e.copy(out=sN.a[:, i, :], in_=P.a), [P], [sN], waw=False)
                        else:
                            mk.op("dve", lambda e, P=P, sN=sN, i=i: e.tensor_copy(out=sN.a[:, i, :], in_=P.a), [P], [sN],
                                  waw=False)
                    elif role == "sig":
                        mk.op("act", lambda e, P=P, sN=sN, i=i: e.activation(out=sN.a[:, i, :], in_=P.a, func=AF.Sigmoid),
                              [P], [sN], waw=False)
                    elif role == "sigk":
                        mk.op("act", lambda e, P=P, i=i: e.activation(out=sbo.a[:, i, :], in_=P.a, func=AF.Sigmoid),
                              [P], [sbo], waw=False)
                    elif role == "silu":
                        tA = ta[e_]
                        mk.op("act", lambda e, P=P, tA=tA: e.activation(out=tA.a, in_=P.a, func=AF.Sigmoid), [P], [tA])
                        mk.op("dve", lambda e, P=P, tA=tA, sN=sN, i=i: e.tensor_tensor(out=sN.a[:, i, :], in0=P.a, in1=tA.a,
                                                                                op=ALU.mult), [P, tA], [sN], waw=False)
                    elif role == "silum":
                        tA, tB = ta[e_], tb2[e_]
                        mk.op("act", lambda e, P=P, tA=tA: e.activation(out=tA.a, in_=P.a, func=AF.Sigmoid), [P], [tA])
                        mk.op("dve", lambda e, P=P, tA=tA, tB=tB: e.tensor_tensor(out=tB.a, in0=P.a, in1=tA.a, op=ALU.mult),
                              [P, tA], [tB])
                        mk.op("pool", lambda e, tB=tB, sN=sN, i=i: e.tensor_tensor(out=sN.a[:, i, :], in0=tB.a,
                                                                                 in1=sbo.a[:, i, :], op=ALU.mult),
                              [tB, sbo], [sN], waw=False)
                    elif role == "gates":
                        ti = st * NTS + i
                        mk.op("dve", lambda e, P=P, ti=ti: e.tensor_copy(out=Graw.a[:, ti, :], in_=P.a[:, 0:16]), [P], [Graw],
                              waw=False)
                    elif role == "qk":
                        tA, s8_, qn_, qr_ = ta[e_], s8[e_], qn[e_], qrot[e_]
                        a1, a2, a3, a4 = t1[e_], t2[e_], t3[e_], t4[e_]
                        rt = RT[name]
                        pQ = psQ[cnt["q"] % 2]
                        cnt["q"] += 1
                        mk.op("act", lambda e, P=P, tA=tA: e.activation(out=tA.a, in_=P.a, func=AF.Square), [P], [tA])
                        mk.op("dve", lambda e, tA=tA, s8_=s8_: e.tensor_reduce(
                            out=s8_.a, in_=tA.a.rearrange("p (g d) -> p g d", g=8), axis=AX.X, op=ALU.add), [tA], [s8_])
                        rsqrt_act(mk, s8_, s8_, 64)
                        mk.op("dve", lambda e, P=P, s8_=s8_, qn_=qn_: e.tensor_tensor(
                            out=qn_.a, in0=P.a.rearrange("p (g d) -> p g d", g=8), in1=bc(s8_.a, 2, [128, 8, 64]),
                            op=ALU.mult), [P, s8_], [qn_])
                        x1 = qn_.a[:, :, 0:32]
                        x2 = qn_.a[:, :, 32:64]

                        def tbv(k_, rt=rt, i=i):
                            return bc(rt.a[:, i, k_, :], 1, [128, 8, 32])

                        mk.op("dve", lambda e, a1=a1, x1=x1, v=tbv(0): e.tensor_tensor(out=a1.a, in0=x1, in1=v, op=ALU.mult),
                              [qn_, rt], [a1])
                        mk.op("dve", lambda e, a2=a2, x2=x2, v=tbv(1): e.tensor_tensor(out=a2.a, in0=x2, in1=v, op=ALU.mult),
                              [qn_, rt], [a2])
                        mk.op("dve", lambda e, a1=a1, a2=a2, qr_=qr_: e.tensor_tensor(out=qr_.a[:, :, 0:32], in0=a1.a, in1=a2.a,
                                                                                 op=ALU.subtract), [a1, a2], [qr_])
                        mk.op("pool", lambda e, a3=a3, x2=x2, v=tbv(2): e.tensor_tensor(out=a3.a, in0=x2, in1=v, op=ALU.mult),
                              [qn_, rt], [a3])
                        mk.op("pool", lambda e, a4=a4, x1=x1, v=tbv(3): e.tensor_tensor(out=a4.a, in0=x1, in1=v, op=ALU.mult),
                              [qn_, rt], [a4])
                        mk.op("pool", lambda e, a3=a3, a4=a4, qr_=qr_: e.tensor_tensor(out=qr_.a[:, :, 32:64], in0=a3.a,
                                                                                  in1=a4.a, op=ALU.add), [a3, a4], [qr_],
                              waw=False)
                        qf = qr_.a.rearrange("p g d -> p (g d)")
                        for hh in range(4):
                            mk.op("pe", lambda e, pQ=pQ, qf=qf, hh=hh: e.transpose(out=pQ.a[:, hh, :],
                                                                                in_=qf[:, hh * 128:(hh + 1) * 128],
                                                                                identity=ident_b.a),
                                  [qr_, ident_b], [pQ], waw=(hh == 0), signal=(hh == 3))
                        mk.op("act", lambda e, pQ=pQ, sT=sT, i=i: e.copy(out=sT.a[:, :, i * 128:(i + 1) * 128], in_=pQ.a),
                              [pQ], [sT], waw=False)
                    elif role == "conv":
                        tA, tB = ta[e_], tb2[e_]
                        pQ = psQ[cnt["q"] % 2]
                        cnt["q"] += 1
                        coff = 0 if name == "q" else 512
                        yb_ = ybf[e_] if name == "q" else None
                        yv = yb_.a if name == "q" else sN.a[:, i, :]
                        yres = yb_ if name == "q" else sN
                        mk.op("dve", lambda e, P=P, tA=tA, coff=coff: e.tensor_tensor(out=tA.a, in0=P.a,
                                                                                   in1=cbb.a[:, coff:coff + 512], op=ALU.add),
                              [P, cbb], [tA])
                        mk.op("act", lambda e, tA=tA, tB=tB: e.activation(out=tB.a, in_=tA.a, func=AF.Sigmoid), [tA], [tB])
                        sc = 0.125 if name == "q" else 1.0
                        mk.op("dve", lambda e, tA=tA, tB=tB, yv=yv, sc=sc: e.scalar_tensor_tensor(
                            out=yv, in0=tA.a, scalar=sc, in1=tB.a, op0=ALU.mult, op1=ALU.mult), [tA, tB], [yres],
                            waw=(name == "q"))
                        for hh in range(4):
                            mk.op("pe", lambda e, pQ=pQ, yv=yv, hh=hh: e.transpose(out=pQ.a[:, hh, :],
                                                                                in_=yv[:, hh * 128:(hh + 1) * 128],
                                                                                identity=ident_b.a),
                                  [yres, ident_b], [pQ], waw=(hh == 0), signal=(hh == 3))
                        mk.op("act", lambda e, pQ=pQ, sT=sT, i=i: e.copy(out=sT.a[:, :, i * 128:(i + 1) * 128], in_=pQ.a),
                              [pQ], [sT], waw=False)
                tsl = slice(tok0, tok0 + TS)
                if role == "qk":
                    dst, rn = (qT_scr, "qT") if name == "q" else (kT_scr, "kT")
                    store(dst[half * 4:(half + 1) * 4, :, tsl].rearrange("h p t -> p h t"), sT.a, [sT], [R[rn]])
                elif role == "conv":
                    dst, rn = (mqT_scr, "mqT") if name == "q" else (mkT_scr, "mkT")
                    store(dst[:, :, tsl].rearrange("h p t -> p h t"), sT.a, [sT], [R[rn]])
                    if name == "k":
                        store(mk_scr[tsl, :].rearrange("(i p) c -> p i c", p=128), sN.a, [sN], [R["mk"]])
                elif role in ("copy", "silu", "silum", "sig"):
                    store(scr[name][tsl, half * 512:(half + 1) * 512].rearrange("(i p) c -> p i c", p=128), sN.a,
                          [sN], [R[name]])
        mk.barrier()
        A.top = m

    def phase2(l):
        m = A.top
        gtb = A.tile([16], F32)
        lamb = A.tile([256], F32)
        G = A.tile([NT, 16], F32)
        SPl = A.tile([NT, 8], F32)
        T1 = A.tile([NT, 8], F32)
        Aal = A.tile([NT, 8], F32)
        Kp = A.tile([NT, 8], F32)
        Binv = A.tile([NT, 8], F32)
        Dec = A.tile([NT, 8], F32)
        neglam = A.tile([1], F32)
        ltmp = A.tile([64], F32)
        ls = A.tile([2], F32)
        load(gtb.a, gate_b[l].partition_broadcast(128), [R["const"]], [gtb])
        load(lamb.a, lam_qk[l].partition_broadcast(128), [R["const"]], [lamb])
        nbp = psb(0, NT * 8)
        nblp = psb(512, NT * 8)
        mk.op("dve", lambda e: e.tensor_tensor(out=G.a, in0=Graw.a, in1=bc(gtb.a, 1, [128, NT, 16]), op=ALU.add),
              [Graw, gtb], [G])
        mk.op("act", lambda e: e.activation(out=SPl.a, in_=G.a[:, :, 8:16], func=AF.Exp, scale=-1.0), [G], [SPl])
        mk.op("act", lambda e: e.activation(out=SPl.a, in_=SPl.a, func=AF.Ln, bias=1.0), [SPl], [SPl])
        spf = SPl.a.rearrange("p c h -> p (c h)")
        mk.op("pe", lambda e: e.matmul(nbp.a, lhsT=tri_f.a, rhs=spf, start=True, stop=True), [tri_f, SPl], [nbp])
        mk.op("pe", lambda e: e.matmul(nblp.a, lhsT=ones_f.a, rhs=spf, start=True, stop=True), [ones_f, SPl], [nblp])
        nb3 = nbp.a.rearrange("p (c h) -> p c h", h=8)
        nbl3 = nblp.a.rearrange("p (c h) -> p c h", h=8)
        mk.op("dve", lambda e: e.tensor_tensor(out=T1.a, in0=nb3, in1=G.a[:, :, 0:8], op=ALU.add), [nbp, G], [T1])
        mk.op("act", lambda e: e.activation(out=Aal.a, in_=T1.a, func=AF.Exp), [T1], [Aal])
        mk.op("dve", lambda e: e.tensor_tensor(out=T1.a, in0=T1.a, in1=nbl3, op=ALU.subtract), [T1, nblp], [T1])
        mk.op("act", lambda e: e.activation(out=Kp.a, in_=T1.a, func=AF.Exp), [T1], [Kp])
        mk.op("act", lambda e: e.activation(out=Binv.a, in_=nb3, func=AF.Exp), [nbp], [Binv])
        mk.op("act", lambda e: e.activation(out=Dec.a, in_=nbl3, func=AF.Exp, scale=-1.0), [nblp], [Dec])
        for i_ in range(2):
            mk.op("dve", lambda e, i_=i_: e.tensor_tensor(out=ltmp.a, in0=lamb.a[:, i_ * 128:i_ * 128 + 64],
                                                          in1=lamb.a[:, i_ * 128 + 64:i_ * 128 + 128], op=ALU.mult),
                  [lamb], [ltmp])
            mk.op("dve", lambda e, i_=i_: e.tensor_reduce(out=ls.a[:, i_:i_ + 1], in_=ltmp.a, axis=AX.X, op=ALU.add),
                  [ltmp], [ls])
        mk.op("act", lambda e: e.activation(out=ls.a, in_=ls.a, func=AF.Exp), [ls], [ls])
        mk.op("dve", lambda e: e.tensor_tensor(out=neglam.a, in0=ls.a[:, 1:2], in1=ls.a[:, 0:1], op=ALU.subtract), [ls], [neglam])
        mk.op("dve", lambda e: e.tensor_scalar(out=neglam.a, in0=neglam.a, scalar1=-float(lam_inits[l]), scalar2=None,
                                               op0=ALU.add), [neglam], [neglam])
        mk.barrier()

        m2 = A.top
        mq = [A.tile([4, 128], BF16) for _ in range(2)]
        mkk = [A.tile([4, 128], BF16) for _ in range(2)]
        mkn = [A.tile([512], BF16) for _ in range(2)]
        Vm = [A.tile([8, 129], BF16) for _ in range(2)]
        ogt = [A.tile([1024], BF16) for _ in range(2)]
        hbS = [A.tile([1024], BF16) for _ in range(2)]
        C32 = [A.tile([129], F32) for _ in range(8)]
        Cb = [A.tile([129], BF16) for _ in range(8)]
        ATm = [A.tile([128], BF16) for _ in range(3)]
        Vk = [A.tile([129], BF16) for _ in range(3)]
        dd = [A.tile([1], F32) for _ in range(3)]
        ssh = [A.tile([1], F32) for _ in range(3)]
        hbf = [A.tile([128], F32) for _ in range(3)]
        junk2 = A.tile([128], BF16)
        ATp = [psb(i * 128, 128) for i in range(4)]
        BRp = [psb(512 + i * 256, 129) for i in range(4)]
        dCp = [psb(1536 + i * 256, 129) for i in range(4)]
        for hh in range(8):
            mk.op("pool", lambda e, hh=hh: e.memset(C32[hh].a, 0.0), [], [C32[hh]])
            mk.op("pool", lambda e, hh=hh: e.memset(Cb[hh].a, 0.0), [], [Cb[hh]])
        for b_ in range(2):
            mk.op("pool", lambda e, b_=b_: e.memset(Vm[b_].a[:, :, 128:129], 1.0), [], [Vm[b_]])
        n = 0
        pendb = [None]
        for c in range(NT):
            cs_ = slice(c * 128, (c + 1) * 128)
            q_, k_, kn_, V_, og_, hS = mq[c % 2], mkk[c % 2], mkn[c % 2], Vm[c % 2], ogt[c % 2], hbS[c % 2]
            load(q_.a, mqT_scr[:, :, cs_].rearrange("a p t -> p a t"), [R["mqT"]], [q_])
            load(k_.a, mkT_scr[:, :, cs_].rearrange("a p t -> p a t"), [R["mkT"]], [k_])
            load(kn_.a, mk_scr[cs_, :], [R["mk"]], [kn_])
            load(V_.a[:, :, 0:128], mv_scr[cs_, :].rearrange("p (h d) -> p h d", h=8), [R["mv"]], [V_], waw=False)
            load(og_.a, og_scr[cs_, :], [R["og"]], [og_])
            for hd in range(8):
                pr = hd // 2
                rows = slice((hd % 2) * 64, (hd % 2) * 64 + 64)
                AT, BR, dC = ATp[n % 4], BRp[n % 4], dCp[n % 4]
                am, vk, dd_, ssh_, hb_ = ATm[n % 3], Vk[n % 3], dd[n % 3], ssh[n % 3], hbf[n % 3]
                n += 1
                C3, Cb_ = C32[hd], Cb[hd]
                mk.op("pe", lambda e, AT=AT, k_=k_, q_=q_, rows=rows, pr=pr: e.matmul(
                    AT.a, lhsT=k_.a[rows, pr, :], rhs=q_.a[rows, pr, :], start=True, stop=True), [k_, q_], [AT])
                mk.op("dve", lambda e, AT=AT, am=am, c=c, hd=hd: e.scalar_tensor_tensor(
                    out=am.a, in0=AT.a, scalar=Aal.a[:, c, hd:hd + 1], in1=tri_b.a, op0=ALU.mult, op1=ALU.mult),
                    [AT, Aal, tri_b], [am])
                mk.op("pool", lambda e, vk=vk, V_=V_, c=c, hd=hd: e.tensor_scalar(
                    out=vk.a, in0=V_.a[:, hd, :], scalar1=Kp.a[:, c, hd:hd + 1], scalar2=None, op0=ALU.mult),
                    [V_, Kp], [vk])

                def stage_b(AT=AT, BR=BR, dC=dC, am=am, vk=vk, dd_=dd_, ssh_=ssh_, hb_=hb_, C3=C3, Cb_=Cb_, q_=q_, k_=k_, kn_=kn_,
                            V_=V_, og_=og_, hS=hS, rows=rows, pr=pr, hd=hd, c=c, last=(hd == 7), cs_=cs_):
                    mk.op("pe", lambda e, BR=BR, am=am, V_=V_, hd=hd: e.matmul(
                        BR.a, lhsT=am.a, rhs=V_.a[:, hd, :], start=True, stop=False), [am, V_], [BR], signal=False)
                    mk.op("pe", lambda e, BR=BR, q_=q_, Cb_=Cb_, rows=rows, pr=pr: e.matmul(
                        BR.a, lhsT=q_.a[rows, pr, :], rhs=Cb_.a[rows, :], start=False, stop=True), [q_, Cb_], [BR], waw=False)
                    mk.op("pe", lambda e, dC=dC, kn_=kn_, vk=vk, pr=pr: e.matmul(
                        dC.a, lhsT=kn_.a[:, pr * 128:(pr + 1) * 128], rhs=vk.a, start=True, stop=True), [kn_, vk], [dC])
                    mk.op("dve", lambda e, C3=C3, dC=dC, rows=rows, c=c, hd=hd: e.scalar_tensor_tensor(
                        out=C3.a[rows, :], in0=C3.a[rows, :], scalar=Dec.a[rows, c, hd:hd + 1], in1=dC.a[rows, :],
                        op0=ALU.mult, op1=ALU.add), [C3, Dec, dC], [C3])
                    mk.op("act", lambda e, C3=C3, Cb_=Cb_, rows=rows: e.copy(out=Cb_.a[rows, :], in_=C3.a[rows, :]), [C3], [Cb_])
                    mk.op("act", lambda e, BR=BR, dd_=dd_: e.activation(out=dd_.a, in_=BR.a[:, 128:129], func=AF.Abs), [BR], [dd_])
                    mk.op("dve", lambda e, dd_=dd_, c=c, hd=hd: e.tensor_tensor(
                        out=dd_.a, in0=dd_.a, in1=Binv.a[:, c, hd:hd + 1], op=ALU.max), [dd_, Binv], [dd_])
                    mk.op("dve", lambda e, dd_=dd_: e.reciprocal(out=dd_.a, in_=dd_.a), [dd_], [dd_])
                    mk.op("dve", lambda e, BR=BR, dd_=dd_, hb_=hb_: e.tensor_scalar(
                        out=hb_.a, in0=BR.a[:, 0:128], scalar1=dd_.a[:, 0:1], scalar2=None, op0=ALU.mult), [BR, dd_], [hb_])
                    mk.op("act", lambda e, hb_=hb_, ssh_=ssh_: e.activation(out=junk2.a, in_=hb_.a, func=AF.Square,
                                                                          accum_out=ssh_.a), [hb_], [junk2, ssh_])
                    rsqrt_act(mk, ssh_, ssh_, 128)
                    mk.op("dve", lambda e, hb_=hb_, ssh_=ssh_, hS=hS, og_=og_, hd=hd: e.scalar_tensor_tensor(
                        out=hS.a[:, hd * 128:(hd + 1) * 128], in0=hb_.a, scalar=ssh_.a[:, 0:1],
                        in1=og_.a[:, hd * 128:(hd + 1) * 128], op0=ALU.mult, op1=ALU.mult), [hb_, ssh_, og_], [hS], waw=False)
                    if last:
                        store(hb_scr[cs_, :], hS.a, [hS], [R["hb"]])
                if PIPE_ML:
                    if pendb[0] is not None:
                        pendb[0]()
                    pendb[0] = stage_b
                else:
                    stage_b()
        if pendb[0] is not None:
            pendb[0]()
        mk.barrier()
        A.top = m2

        qTb = [A.tile([S], BF16) for _ in range(2)]
        kTb = [A.tile([S], BF16) for _ in range(2)]
        Vb = [A.tile([NT, 128], BF16) for _ in range(2)]
        zsb = [A.tile([4, 128], BF16) for _ in range(2)]
        Eb = [A.tile([2, 512], BF16) for _ in range(3)]
        ones_b = A.tile([128], BF16)
        rd = [A.tile([512], F32) for _ in range(2)]
        o0 = A.tile([512], F32)
        t1_ = A.tile([512], F32)
        dT = [A.tile([512], F32) for _ in range(2)]
        sq4 = A.tile([4, 128], F32)
        ss4 = A.tile([4], F32)
        tq = A.tile([4, 128], F32)
        oaS = [A.tile([4, 128], BF16) for _ in range(2)]
        STb = [psb(i * 1024, 1024) for i in range(2)]
        accs = [psb(2048 + r_ * 512, 512) for r_ in range(2)]
        dens = [psb(3072 + r_ * 512, 512) for r_ in range(2)]
        mk.op("pool", lambda e: e.memset(ones_b.a, 1.0), [], [ones_b])
        gc = [0]
        pend = [None]

        def flush():
            if pend[0] is not None:
                f = pend[0]
                pend[0] = None
                f()

        def stage_c(E, Vg, kt, c0, first, last, tail):
            for r_ in range(2):
                mk.op("pe", lambda e, r_=r_: e.matmul(accs[r_].a[:, c0:512], lhsT=Vg.a[:, kt, :], rhs=E.a[:, r_, c0:512],
                                                      start=first, stop=last), [E, Vg], [accs[r_]], waw=first)
            for r_ in range(2):
                mk.op("pe", lambda e, r_=r_: e.matmul(dens[r_].a[:, c0:512], lhsT=ones_b.a, rhs=E.a[:, r_, c0:512],
                                                      start=first, stop=last), [E, ones_b], [dens[r_]], waw=first)
            if tail is not None:
                tail()

        def epilogue(oS, zs, qsl, hd, d_):
            for r_ in range(2):
                mk.op("dve", lambda e, r_=r_: e.reciprocal(out=rd[r_].a, in_=dens[r_].a), [dens[r_]], [rd[r_]])
            mk.op("dve", lambda e: e.tensor_tensor(out=o0.a, in0=accs[0].a, in1=rd[0].a, op=ALU.mult), [accs[0], rd[0]], [o0])
            mk.op("dve", lambda e: e.tensor_tensor(out=t1_.a, in0=accs[1].a, in1=rd[1].a, op=ALU.mult), [accs[1], rd[1]], [t1_])
            mk.op("dve", lambda e, d_=d_: e.scalar_tensor_tensor(out=d_.a, in0=t1_.a, scalar=neglam.a[:, 0:1], in1=o0.a,
                                                                op0=ALU.mult, op1=ALU.add), [t1_, neglam, o0], [d_])
            T = STb[gc[0] % 2]
            gc[0] += 1
            Tv = T.a[:, 0:512].rearrange("p (j d) -> p j d", j=4)
            for j in range(4):
                mk.op("pe", lambda e, j=j, d_=d_, Tv=Tv: e.transpose(out=Tv[:, j, :], in_=d_.a[:, j * 128:(j + 1) * 128],
                                                                    identity=ident_f.a), [d_, ident_f], [T],
                      waw=(j == 0), signal=(j == 3))
            mk.op("act", lambda e, Tv=Tv: e.activation(out=sq4.a, in_=Tv, func=AF.Square), [T], [sq4])
            mk.op("dve", lambda e: e.tensor_reduce(out=ss4.a, in_=sq4.a, axis=AX.X, op=ALU.add), [sq4], [ss4])
            rsqrt_act(mk, ss4, ss4, 128)
            mk.op("dve", lambda e, Tv=Tv: e.tensor_tensor(out=tq.a, in0=Tv, in1=bc(ss4.a, 2, [128, 4, 128]), op=ALU.mult),
                  [T, ss4], [tq])
            mk.op("pool", lambda e, oS=oS, zs=zs: e.tensor_tensor(out=oS.a, in0=tq.a, in1=zs.a, op=ALU.mult), [tq, zs], [oS])
            store(oa_scr[qsl, hd * 128:(hd + 1) * 128].rearrange("(j p) d -> p j d", p=128), oS.a, [oS], [R["oa"]])

        for hd in range(8):
            qT, kT, Vg = qTb[hd % 2], kTb[hd % 2], Vb[hd % 2]
            load(qT.a, qT_scr[hd], [R["qT"]], [qT])
            load(kT.a, kT_scr[hd], [R["kT"]], [kT])
            load(Vg.a, v_scr[:, hd * 128:(hd + 1) * 128].rearrange("(c p) d -> p c d", p=128), [R["v"]], [Vg])
            for Q in range(NQ):
                zs = zsb[(hd * NQ + Q) % 2]
                oS = oaS[(hd * NQ + Q) % 2]
                d_ = dT[(hd * NQ + Q) % 2]
                qsl = slice(Q * 512, (Q + 1) * 512)
                load(zs.a, zsa_scr[qsl, hd * 128:(hd + 1) * 128].rearrange("(j p) d -> p j d", p=128), [R["zsa"]], [zs])
                nk = 4 * (Q + 1)
                for kt in range(nk):
                    ir = kt - 4 * Q
                    c0 = max(ir, 0) * 128
                    ST = STb[gc[0] % 2]
                    E = Eb[gc[0] % 3]
                    gc[0] += 1
                    for r_ in range(2):
                        rows = slice(r_ * 64, (r_ + 1) * 64)
                        mk.op("pe", lambda e, ST=ST, kT=kT, qT=qT, rows=rows, kt=kt, r_=r_, Q=Q, c0=c0: e.matmul(
                            ST.a[:, r_ * 512 + c0:(r_ + 1) * 512], lhsT=kT.a[rows, kt * 128:(kt + 1) * 128],
                            rhs=qT.a[rows, Q * 512 + c0:(Q + 1) * 512], start=True, stop=True), [kT, qT], [ST],
                            waw=(r_ == 0), signal=(r_ == 1))
                    STv = ST.a.rearrange("p (r c) -> p r c", r=2)[:, :, c0:512]
                    mk.op("act", lambda e, STv=STv, E=E, c0=c0: e.activation(out=E.a[:, :, c0:512], in_=STv, func=AF.Exp,
                                                                            scale=0.125), [ST], [E])
                    if ir >= 0:
                        mk.op("pool", lambda e, E=E, c0=c0: e.tensor_tensor(
                            out=E.a[:, :, c0:c0 + 128], in0=E.a[:, :, c0:c0 + 128], in1=bc(tri_b.a, 1, [128, 2, 128]),
                            op=ALU.mult), [E, tri_b], [E])
                    flush()
                    tail = None
                    if kt == nk - 1:
                        def tail(oS=oS, zs=zs, qsl=qsl, hd=hd, d_=d_):
                            epilogue(oS, zs, qsl, hd, d_)
                    pend[0] = (lambda E=E, Vg=Vg, kt=kt, c0=c0, first=(kt == 0), last=(kt == nk - 1), tail=tail:
                               stage_c(E, Vg, kt, c0, first, last, tail))
        flush()
        mk.barrier()
        A.top = m

    def phase3(l):
        m = A.top
        xsrc, Rx = (x_in, R["x_in"]) if l == 0 else (x_mid, R["x_mid"])
        dst, Rd = (x_mid, R["x_mid"]) if l < L - 1 else (out_d, R["out"])
        Rwo = R["wobf%d" % l]
        W3 = [A.tile([8, 1024], BF16) for _ in range(3)]
        for mi in range(3):
            load(W3[mi].a, wobf[l][mi], [Rwo], [W3[mi]])
        oat = [A.tile([1024], BF16) for _ in range(2)]
        hbt = [A.tile([1024], BF16) for _ in range(2)]
        sga = [A.tile([1024], BF16) for _ in range(2)]
        sgb = [A.tile([1024], BF16) for _ in range(2)]
        xr = [A.tile([1024], F32) for _ in range(2)]
        oaT = [A.tile([8, 128], BF16) for _ in range(2)]
        hbT = [A.tile([8, 128], BF16) for _ in range(2)]
        uT = [A.tile([8, 128], BF16) for _ in range(2)]
        u1 = [A.tile([1024], F32) for _ in range(2)]
        u2 = [A.tile([1024], F32) for _ in range(2)]
        ub = [A.tile([1024], BF16) for _ in range(2)]
        ot = [A.tile([1024], F32) for _ in range(2)]
        Ya = psb(0, 1024)
        Yb = psb(1024, 1024)
        Yo = psb(2048, 1024)
        pT = [psb(3072 + i * 512, 512, BF16, [8, 128]) for i in range(2)]
        np_ = 0
        for t in range(NT):
            b = t % 2
            ts_ = slice(t * 128, (t + 1) * 128)
            load(oat[b].a, oa_scr[ts_, :], [R["oa"]], [oat[b]])
            load(hbt[b].a, hb_scr[ts_, :], [R["hb"]], [hbt[b]])
            load(sga[b].a, sga_scr[ts_, :], [R["sga"]], [sga[b]])
            load(sgb[b].a, sgb_scr[ts_, :], [R["sgb"]], [sgb[b]])
            load(xr[b].a, xsrc[ts_, :], [Rx], [xr[b]])
            for (src, dstT, eng) in ((oat[b], oaT[b], "act"), (hbt[b], hbT[b], "dve")):
                p_ = pT[np_ % 2]
                np_ += 1
                for kc in range(8):
                    mk.op("pe", lambda e, p_=p_, src=src, kc=kc: e.transpose(out=p_.a[:, kc, :],
                                                                          in_=src.a[:, kc * 128:(kc + 1) * 128],
                                                                          identity=ident_b.a), [src, ident_b], [p_],
                          waw=(kc == 0), signal=(kc == 7))
                if eng == "act":
                    mk.op("act", lambda e, p_=p_, dstT=dstT: e.copy(out=dstT.a, in_=p_.a), [p_], [dstT])
                else:
                    mk.op("dve", lambda e, p_=p_, dstT=dstT: e.tensor_copy(out=dstT.a, in_=p_.a), [p_], [dstT])
            for (Y, lT, W) in ((Ya, oaT[b], W3[0]), (Yb, hbT[b], W3[1])):
                for half in range(2):
                    for kc in range(8):
                        c = half * 8 + kc
                        mk.op("pe", lambda e, Y=Y, lT=lT, W=W, half=half, kc=kc: e.matmul(
                            Y.a[:, half * 512:(half + 1) * 512], lhsT=lT.a[:, kc, :], rhs=W.a[:, kc, half * 512:(half + 1) * 512],
                            start=(kc == 0), stop=(kc == 7)), [lT, W], [Y], waw=(c == 0), signal=(c == 15))
            mk.op("dve", lambda e, b=b: e.tensor_tensor(out=u1[b].a, in0=Ya.a, in1=sga[b].a, op=ALU.mult), [Ya, sga[b]], [u1[b]])
            mk.op("dve", lambda e, b=b: e.tensor_tensor(out=u2[b].a, in0=Yb.a, in1=sgb[b].a, op=ALU.mult), [Yb, sgb[b]], [u2[b]])
            mk.op("pool", lambda e, b=b: e.tensor_tensor(out=ub[b].a, in0=u1[b].a, in1=u2[b].a, op=ALU.add),
                  [u1[b], u2[b]], [ub[b]])
            p_ = pT[np_ % 2]
            np_ += 1
            for kc in range(8):
                mk.op("pe", lambda e, p_=p_, b=b, kc=kc: e.transpose(out=p_.a[:, kc, :], in_=ub[b].a[:, kc * 128:(kc + 1) * 128],
                                                                  identity=ident_b.a), [ub[b], ident_b], [p_],
                      waw=(kc == 0), signal=(kc == 7))
            mk.op("act", lambda e, p_=p_, b=b: e.copy(out=uT[b].a, in_=p_.a), [p_], [uT[b]])
            for half in range(2):
                for kc in range(8):
                    c = half * 8 + kc
                    mk.op("pe", lambda e, b=b, half=half, kc=kc: e.matmul(
                        Yo.a[:, half * 512:(half + 1) * 512], lhsT=uT[b].a[:, kc, :],
                        rhs=W3[2].a[:, kc, half * 512:(half + 1) * 512], start=(kc == 0), stop=(kc == 7)),
                        [uT[b], W3[2]], [Yo], waw=(c == 0), signal=(c == 15))
            mk.op("dve", lambda e, b=b: e.tensor_tensor(out=ot[b].a, in0=Yo.a, in1=xr[b].a, op=ALU.add), [Yo, xr[b]], [ot[b]])
            store(dst[ts_, :], ot[b].a, [ot[b]], [Rd])
        mk.barrier()
        A.top = m

    for l in range(L):
        phase0(l)
    for l in range(L):
        phase1(l)
        phase2(l)
        phase3(l)
    mk.emit()
    return nc, mk


def _lam_init(l):
    return 0.8 - 0.6 * math.exp(-0.3 * l)


def make_in_map(xb, posb, S, norm_g, w_in, q_norm_g, k_norm_g, lambda_qk, attn_norm_g, w_out_a, conv_w, conv_b, igate_b,
                fgate_b, mlstm_norm_g, w_out_b, w_o):
    L = w_in.shape[0]
    NT = S // 128
    perm = np.concatenate([np.arange(0, 4096), np.arange(5120, 6144), np.arange(6160, 7184), np.arange(7184, 8208),
                           np.arange(8208, 9232), np.arange(9232, 10256), np.arange(4096, 5120), np.arange(6144, 6160)])
    f32 = np.float32
    inv_freq = (10000.0 ** (-np.arange(0, 64, 2, dtype=np.float32) / 64)).astype(np.float32)
    freq = np.tile((inv_freq.astype(np.float64) / (2.0 * math.pi)).astype(f32)[None, :], (128, 1))
    return {
        "x": np.ascontiguousarray(xb, dtype=f32),
        "pos": np.ascontiguousarray(posb.reshape(NT, 128).T.astype(np.int32)),
        "c_ident": np.eye(128, dtype=f32),
        "c_tri": np.triu(np.ones((128, 128), dtype=f32)),
        "c_freq": np.ascontiguousarray(freq),
        "w_in": np.ascontiguousarray(w_in[:, :, perm], dtype=f32),
        "w_out": np.ascontiguousarray(np.stack([w_out_a, w_out_b, w_o], axis=1), dtype=f32),
        "norm_g": np.ascontiguousarray(norm_g.reshape(L, 8, 128).transpose(0, 2, 1), dtype=f32),
        "q_g": np.ascontiguousarray(q_norm_g, dtype=f32),
        "k_g": np.ascontiguousarray(k_norm_g, dtype=f32),
        "lam_qk": np.ascontiguousarray(lambda_qk.reshape(L, 256), dtype=f32),
        "attn_g": np.ascontiguousarray(attn_norm_g.reshape(L, 128, 1), dtype=f32),
        "ml_g": np.ascontiguousarray(mlstm_norm_g.reshape(L, 128, 1), dtype=f32),
        "conv_w": np.ascontiguousarray(conv_w, dtype=f32),
        "conv_b": np.ascontiguousarray(conv_b, dtype=f32),
        "gate_b": np.ascontiguousarray(np.concatenate([igate_b, fgate_b], axis=1), dtype=f32),
    }


def kernel(x, positions, norm_g, w_in, q_norm_g, k_norm_g, lambda_qk, attn_norm_g, w_out_a, conv_w, conv_b, igate_b, fgate_b,
           mlstm_norm_g, w_out_b, w_o):
    x = np.asarray(x)
    positions = np.asarray(positions)
    Bsz, S, _ = x.shape
    L = int(np.asarray(w_in).shape[0])
    args = [np.asarray(a) for a in (norm_g, w_in, q_norm_g, k_norm_g, lambda_qk, attn_norm_g, w_out_a, conv_w, conv_b,
                                    igate_b, fgate_b, mlstm_norm_g, w_out_b, w_o)]
    nc, _ = build(S, L, [_lam_init(l) for l in range(L)])
    n = 8
    in_maps = [make_in_map(x[c % Bsz], positions[c % Bsz], S, *args) for c in range(n)]
    res = run_bass_kernel_spmd(nc, in_maps, core_ids=list(range(n)))
    out = np.stack([np.asarray(res.results[b]["out"], dtype=np.float32) for b in range(Bsz)], axis=0)
    return out
```
